# Optimizing a Trainium2 kernel written in Bass

```python
import jax, jax.numpy as jnp
from jax import lax
import numpy as np

D_MODEL = 2048
BATCH = 4
SEQ = 4096
DEPTH = 2

RET_HEADS = 8
RET_DK = 64
RET_DV = 128
RET_CHUNK = 128
MLA_HEADS = 8
MLA_Q_RANK = 512
MLA_KV_RANK = 256
MLA_NOPE = 128
MLA_ROPE = 64
MLA_DV = 128
ATTN_BLOCK = 128
RET_WIDTH = RET_HEADS * RET_DV
MLA_WIDTH = MLA_HEADS * MLA_DV
MIX_WIDTH = RET_WIDTH + MLA_WIDTH
IN_SIZES = (RET_HEADS * RET_DK, RET_HEADS * RET_DK, RET_WIDTH, RET_WIDTH, MLA_Q_RANK, MLA_KV_RANK, MLA_ROPE)
IN_COLS = 2 * RET_HEADS * RET_DK + 2 * RET_WIDTH + MLA_Q_RANK + MLA_KV_RANK + MLA_ROPE
D_FF = 256 * ((8 * D_MODEL // 3 + 255) // 256)
ROPE_DIM = 64
ROPE_BASE = 10000.0
EPS = 1e-6
N_MOD = 9

kernel_name = "hybrid_retention_mla_macaron_adaln"


def rmsnorm(x, gain):
    xf = x.astype(jnp.float32)
    y = xf * lax.rsqrt(jnp.mean(xf * xf, axis=-1, keepdims=True) + EPS)
    return (y * gain.astype(jnp.float32)).astype(x.dtype)


def rope_tables(positions):
    inv = ROPE_BASE ** (-jnp.arange(0, ROPE_DIM, 2, dtype=jnp.float32) / ROPE_DIM)
    ang = positions.astype(jnp.float32)[..., None] * inv
    return jnp.cos(ang), jnp.sin(ang)


def apply_rope(x, cos, sin):
    x1, x2 = jnp.split(x, 2, axis=-1)
    c = cos[:, :, None, :].astype(x.dtype)
    s = sin[:, :, None, :].astype(x.dtype)
    return jnp.concatenate([x1 * c - x2 * s, x1 * s + x2 * c], axis=-1)


def modulate(h, shift, scale):
    return h * (1 + scale[:, None, :]) + shift[:, None, :]


def swiglu(h, w_gu, w_down):
    g, u = jnp.split(h @ w_gu, 2, axis=-1)
    return (jax.nn.silu(g) * u) @ w_down


def chunkwise_retention(q, k, v):
    B, S, H, dk = q.shape
    dv = v.shape[-1]
    C = RET_CHUNK
    N = S // C
    dt = q.dtype
    log_g = jnp.log1p(-(2.0 ** (-5.0 - jnp.arange(H, dtype=jnp.float32))))
    idx = jnp.arange(C, dtype=jnp.float32)
    rel = idx[:, None] - idx[None, :]
    intra_decay = jnp.where(rel >= 0, jnp.exp(log_g[:, None, None] * jnp.maximum(rel, 0.0)), 0.0)
    q_decay = jnp.exp(log_g[:, None] * (idx + 1.0))
    k_decay = jnp.exp(log_g[:, None] * (C - 1.0 - idx))
    chunk_decay = jnp.exp(log_g * C)

    qc = q.reshape(B, N, C, H, dk)
    kc = k.reshape(B, N, C, H, dk)
    vc = v.reshape(B, N, C, H, dv)
    s = jnp.einsum('bnihd,bnjhd->bnhij', qc, kc) * intra_decay.astype(dt)
    intra = jnp.einsum('bnhij,bnjhe->bnihe', s, vc)
    inc = jnp.einsum('bnjhd,hj,bnjhe->nbhde', kc, k_decay.astype(dt), vc)
    cd = chunk_decay.astype(dt)[None, :, None, None]

    def step(state, inc_n):
        return state * cd + inc_n, state

    _, prev = lax.scan(step, jnp.zeros((B, H, dk, dv), dt), inc)
    cross = jnp.einsum('bnihd,hi,nbhde->bnihe', qc, q_decay.astype(dt), prev)
    return (intra + cross).reshape(B, S, H, dv)


def causal_block_attention(q, k, v):
    B, S, H, dqk = q.shape
    dv = v.shape[-1]
    NB = S // ATTN_BLOCK
    scale = dqk ** -0.5
    qb = q.reshape(B, NB, ATTN_BLOCK, H, dqk).transpose(1, 0, 2, 3, 4)
    key_pos = jnp.arange(S)

    def one_block(args):
        q_blk, start = args
        sc = jnp.einsum('bqhd,bkhd->bhqk', q_blk, k, preferred_element_type=jnp.float32) * scale
        qpos = start + jnp.arange(ATTN_BLOCK)
        sc = jnp.where(key_pos[None, :] <= qpos[:, None], sc, -jnp.inf)
        p = jax.nn.softmax(sc, axis=-1).astype(v.dtype)
        return jnp.einsum('bhqk,bkhe->bqhe', p, v)

    out = lax.map(one_block, (qb, jnp.arange(NB) * ATTN_BLOCK))
    return out.transpose(1, 0, 2, 3, 4).reshape(B, S, H, dv)


def hybrid_mixer(h, cos, sin, w_in, ret_norm_g, q_norm_g, w_uq, kv_norm_g, w_ukv, w_out):
    B, S, _ = h.shape
    proj = h @ w_in
    cuts = [int(v) for v in np.cumsum(IN_SIZES)[:-1]]
    q_r, k_r, v_r, g_r, c_q, c_kv, k_pe = jnp.split(proj, cuts, axis=-1)

    q_r = apply_rope(q_r.reshape(B, S, RET_HEADS, RET_DK), cos, sin)
    k_r = apply_rope(k_r.reshape(B, S, RET_HEADS, RET_DK), cos, sin) * (RET_DK ** -0.5)
    v_r = v_r.reshape(B, S, RET_HEADS, RET_DV)
    y_r = chunkwise_retention(q_r, k_r, v_r)
    y_r = rmsnorm(y_r, ret_norm_g.reshape(RET_HEADS, RET_DV)).reshape(B, S, RET_WIDTH)
    y_r = jax.nn.silu(g_r) * y_r

    q = (rmsnorm(c_q, q_norm_g) @ w_uq).reshape(B, S, MLA_HEADS, MLA_NOPE + MLA_ROPE)
    q_nope, q_pe = jnp.split(q, [MLA_NOPE], axis=-1)
    q_pe = apply_rope(q_pe, cos, sin)
    kv = (rmsnorm(c_kv, kv_norm_g) @ w_ukv).reshape(B, S, MLA_HEADS, MLA_NOPE + MLA_DV)
    k_nope, v_m = jnp.split(kv, [MLA_NOPE], axis=-1)
    k_pe = apply_rope(k_pe[:, :, None, :], cos, sin)
    q_m = jnp.concatenate([q_nope, q_pe], axis=-1)
    k_m = jnp.concatenate([k_nope, jnp.broadcast_to(k_pe, (B, S, MLA_HEADS, MLA_ROPE))], axis=-1)
    y_m = causal_block_attention(q_m, k_m, v_m).reshape(B, S, MLA_WIDTH)

    return jnp.concatenate([y_r, y_m], axis=-1) @ w_out


def setup_inputs(seed: int = 0) -> dict:
    key = jax.random.key(seed)
    ks = jax.random.split(key, 24)
    f32 = jnp.float32

    def nrm(k, shape, scale):
        return jax.random.normal(k, shape, f32) * scale

    def gain(k, shape):
        return 1.0 + 0.02 * jax.random.normal(k, shape, f32)

    offsets = jax.random.randint(ks[2], (BATCH, 1), 0, 1024, dtype=jnp.int32)
    positions = offsets + jnp.arange(SEQ, dtype=jnp.int32)[None, :]
    return {
        "x": nrm(ks[0], (BATCH, SEQ, D_MODEL), 1.0),
        "c": nrm(ks[1], (BATCH, D_MODEL), 1.0),
        "positions": positions,
        "w_ada": nrm(ks[3], (DEPTH, D_MODEL, N_MOD * D_MODEL), 0.5 * D_MODEL ** -0.5),
        "b_ada": nrm(ks[4], (DEPTH, N_MOD * D_MODEL), 0.02),
        "norm_ffn1": gain(ks[5], (DEPTH, D_MODEL)),
        "ffn1_w_gu": nrm(ks[6], (DEPTH, D_MODEL, 2 * D_FF), D_MODEL ** -0.5),
        "ffn1_w_down": nrm(ks[7], (DEPTH, D_FF, D_MODEL), D_FF ** -0.5),
        "norm_mix": gain(ks[8], (DEPTH, D_MODEL)),
        "w_in": nrm(ks[9], (DEPTH, D_MODEL, IN_COLS), D_MODEL ** -0.5),
        "ret_norm_g": gain(ks[10], (DEPTH, RET_WIDTH)),
        "q_norm_g": gain(ks[11], (DEPTH, MLA_Q_RANK)),
        "w_uq": nrm(ks[12], (DEPTH, MLA_Q_RANK, MLA_HEADS * (MLA_NOPE + MLA_ROPE)), MLA_Q_RANK ** -0.5),
        "kv_norm_g": gain(ks[13], (DEPTH, MLA_KV_RANK)),
        "w_ukv": nrm(ks[14], (DEPTH, MLA_KV_RANK, MLA_HEADS * (MLA_NOPE + MLA_DV)), MLA_KV_RANK ** -0.5),
        "w_out": nrm(ks[15], (DEPTH, MIX_WIDTH, D_MODEL), MIX_WIDTH ** -0.5),
        "norm_ffn2": gain(ks[16], (DEPTH, D_MODEL)),
        "ffn2_w_gu": nrm(ks[17], (DEPTH, D_MODEL, 2 * D_FF), D_MODEL ** -0.5),
        "ffn2_w_down": nrm(ks[18], (DEPTH, D_FF, D_MODEL), D_FF ** -0.5),
        "final_norm": gain(ks[19], (D_MODEL,)),
    }


def reference(x, c, positions, w_ada, b_ada, norm_ffn1, ffn1_w_gu, ffn1_w_down, norm_mix, w_in,
              ret_norm_g, q_norm_g, w_uq, kv_norm_g, w_ukv, w_out, norm_ffn2, ffn2_w_gu,
              ffn2_w_down, final_norm):
    cos, sin = rope_tables(positions)
    c_act = jax.nn.silu(c)
    for l in range(DEPTH):
        mod = c_act @ w_ada[l] + b_ada[l]
        sh1, sc1, g1, sh2, sc2, g2, sh3, sc3, g3 = jnp.split(mod, N_MOD, axis=-1)
        h = modulate(rmsnorm(x, norm_ffn1[l]), sh1, sc1)
        x = x + 0.5 * g1[:, None, :] * swiglu(h, ffn1_w_gu[l], ffn1_w_down[l])
        h = modulate(rmsnorm(x, norm_mix[l]), sh2, sc2)
        x = x + g2[:, None, :] * hybrid_mixer(h, cos, sin, w_in[l], ret_norm_g[l], q_norm_g[l], w_uq[l],
                                              kv_norm_g[l], w_ukv[l], w_out[l])
        h = modulate(rmsnorm(x, norm_ffn2[l]), sh3, sc3)
        x = x + 0.5 * g3[:, None, :] * swiglu(h, ffn2_w_gu[l], ffn2_w_down[l])
    return rmsnorm(x, final_norm)
```

```python
import contextlib
import numpy as np
import concourse.bass as bass
import concourse.mybir as mybir
from concourse.bass_utils import run_bass_kernel_spmd

F32 = mybir.dt.float32
BF16 = mybir.dt.bfloat16
I32 = mybir.dt.int32
AF = mybir.ActivationFunctionType
ALU = mybir.AluOpType

D = 2048
NCH = 16
DFF = 5632
NFF = 44
TOK = 2048
TB = 512
NBLK = TOK // TB
EPS = 1e-6
L = 2
NMOD = 144
IN_COLS = 3904


PSUM_NAMES = {"pm", "pst", "pg", "pu", "po", "pa", "pb", "pv", "pt", "pS", "pinc", "pss", "aS", "ay", "ad", "ak", "opo"}


class Buf:
    __slots__ = ("name", "w", "rs", "dsem", "dcount", "psum")

    def __init__(self, name):
        self.name = name
        self.psum = name.rstrip("0123456789") in PSUM_NAMES
        self.w = None
        self.rs = {}
        self.dsem = None
        self.dcount = 0


class Sched:
    ENG = ("pe", "act", "dve", "pool", "sp")

    def __init__(self, nc, stack):
        self.nc = nc
        self.stack = stack
        self.e = {"pe": nc.tensor, "act": nc.scalar, "dve": nc.vector, "pool": nc.gpsimd, "sp": nc.sync}
        self.sems = {}
        self.cur = {}
        self.known = {k: {} for k in self.ENG}
        for k in self.ENG:
            self._newsem(k)
        self.ndsem = 0
        self.dma_bufs = []
        self.free_dsems = []

    def _newsem(self, key):
        self.sems[key] = self.stack.enter_context(self.nc.semaphore("s_" + key))
        self.cur[key] = 0

    def buf(self, name):
        return Buf(name)

    def bufs(self, name, n):
        return [Buf(f"{name}{i}") for i in range(n)]

    def _wait(self, eng, deps):
        need = {}
        for d in deps:
            if d is None:
                continue
            k, v = d
            if k == eng and eng == "pe":
                continue
            if v > need.get(k, 0):
                need[k] = v
        kn = self.known[eng]
        for k, v in need.items():
            if kn.get(k, 0) >= v:
                continue
            self.e[eng].wait_ge(self.sems[k], v)
            kn[k] = v

    def _deps(self, reads, writes):
        deps = []
        for b in reads:
            deps.append(b.w)
            if b.psum:
                deps.extend(b.rs.items())
        for b in writes:
            deps.append(b.w)
            deps.extend(b.rs.items())
        return deps

    def _commit(self, ev, reads, writes):
        k, v = ev
        for b in reads:
            if b.rs.get(k, 0) < v:
                b.rs[k] = v
        for b in writes:
            b.w = ev
            b.rs = {}

    def op(self, eng, fn, reads=(), writes=()):
        self._wait(eng, self._deps(reads, writes))
        ins = fn(self.e[eng])
        self.cur[eng] += 1
        ins.then_inc(self.sems[eng], 1)
        self._commit((eng, self.cur[eng]), reads, writes)

    def mm(self, mms, reads=(), writes=()):
        self._wait("pe", self._deps(reads, writes))
        pe = self.e["pe"]
        ins = None
        for (o, a, b, st, sp) in mms:
            ins = pe.matmul(o, lhsT=a, rhs=b, start=st, stop=sp)
        self.cur["pe"] += 1
        ins.then_inc(self.sems["pe"], 1)
        self._commit(("pe", self.cur["pe"]), reads, writes)

    def pe_ops(self, fns, reads=(), writes=()):
        self._wait("pe", self._deps(reads, writes))
        pe = self.e["pe"]
        ins = None
        for f in fns:
            ins = f(pe)
        self.cur["pe"] += 1
        ins.then_inc(self.sems["pe"], 1)
        self._commit(("pe", self.cur["pe"]), reads, writes)

    def dma(self, q, out, in_, reads=(), writes=(), sem=None):
        if sem is None:
            sem = writes[0] if writes else reads[0]
        if sem.dsem is None:
            if self.free_dsems:
                key = self.free_dsems.pop()
            else:
                key = f"d{self.ndsem}"
                self.ndsem += 1
                self._newsem(key)
            sem.dsem = key
            self.dma_bufs.append(sem)
        key = sem.dsem
        deps = self._deps(reads, writes)
        if self.cur[key] > 0:
            deps.append((key, self.cur[key]))
        self._wait(q, deps)
        ins = self.e[q].dma_start(out=out, in_=in_)
        self.cur[key] += 16
        ins.then_inc(self.sems[key], 16)
        self._commit((key, self.cur[key]), reads, writes)

    def barrier(self):
        if "bar" not in self.sems:
            self._newsem("bar")
        deps = [(k, v) for k, v in self.cur.items() if k not in ("bar", "sp") and v > 0]
        self._wait("sp", deps)
        self.cur["bar"] += 1
        self.e["sp"].sem_inc(self.sems["bar"], 1)
        for eng in self.ENG:
            if eng == "sp":
                continue
            self._wait(eng, [("bar", self.cur["bar"])])
        for eng in self.ENG:
            for k, v in self.cur.items():
                self.known[eng][k] = v
        for b in self.dma_bufs:
            self.free_dsems.append(b.dsem)
            b.dsem = None
        self.dma_bufs = []

    def finish(self, outs):
        deps = [(k, v) for k, v in self.cur.items() if k not in ("bar", "sp") and v > 0]
        self._wait("sp", deps)


class Stream:
    def __init__(self, S, q, loads):
        self.S, self.q, self.loads, self.nxt = S, q, loads, 0
        self.done = {}
        self.ridx = []
        cnt = {}
        for ld in loads:
            r = ld[3]
            self.ridx.append(cnt.get(r, 0))
            cnt[r] = cnt.get(r, 0) + 1

    def ensure(self, i):
        i = min(i, len(self.loads) - 1)
        while self.nxt <= i:
            buf, out_ap, in_ap, ring, nslots = self.loads[self.nxt]
            if self.ridx[self.nxt] >= self.done.get(ring, 0) + nslots:
                break
            self.S.dma(self.q, out_ap, in_ap, writes=[buf])
            self.nxt += 1

    def consumed(self, ring):
        self.done[ring] = self.done.get(ring, 0) + 1


class Prog:
    def __init__(self, nc, stack, layers, T=TOK):
        self.nc, self.stack = nc, stack
        self.S = Sched(nc, stack)
        self.layers = layers
        self.T = T
        self.nblk = T // TB

    _uid = 0

    def sb(self, st, name, shape, dt):
        Prog._uid += 1
        return st.enter_context(self.nc.sbuf_tensor(f"{name}_s{Prog._uid}", shape, dt))

    def ps(self, st, name, shape, dt=F32):
        Prog._uid += 1
        return st.enter_context(self.nc.psum_tensor(f"{name}_p{Prog._uid}", shape, dt))

    def consts(self):
        S, st = self.S, self.stack
        self.ones_bf = self.sb(st, "ones_bf", [128, 128], BF16)
        self.b_ones = S.buf("ones")
        S.op("dve", lambda e: e.memset(self.ones_bf[:], 1.0), writes=[self.b_ones])
        nl = len(self.layers)
        self.mc = self.sb(st, "mc", [128, nl, NMOD], F32)
        self.gs = self.sb(st, "gs", [128, nl, 48], F32)
        self.gh = self.sb(st, "gh", [128, nl, 48], F32)
        self.b_mod = S.buf("mod")
        self.eps_col = self.sb(st, "eps_col", [128, 1], F32)
        S.op("dve", lambda e: e.memset(self.eps_col[:], EPS), writes=[self.b_ones])

    def adaln(self, ccol_all, onehot, w_ada_r, b_ada_c, ng_c, ada_in, ada_g, ncores, stages):
        S, nc = self.S, self.nc
        nl = len(self.layers)
        NJc = NMOD // ncores
        JG = 3
        ngrp = NJc // JG
        NWA = 4
        with contextlib.ExitStack() as st:
            cc = self.sb(st, "cc", [128, 16, 4], F32)
            cact = self.sb(st, "cact", [128, 16, 4], BF16)
            sig = self.sb(st, "sig", [128, 16, 4], F32)
            oh = self.sb(st, "oh", [128, 4], F32)
            bada = self.sb(st, "bada", [128, nl, NMOD], F32)
            ngs = self.sb(st, "ngs", [128, nl, 48], F32)
            modp = self.sb(st, "modp", [128, nl, NJc * 4], F32)
            modg = self.sb(st, "modg", [128, nl, ncores, NJc, 4], F32)
            wa = [self.sb(st, f"wa{i}", [128, 16, JG * 128], BF16) for i in range(NWA)]
            pm = [self.ps(st, f"pm{i}", [128, 512], F32) for i in range(nl)]
            b_cc, b_cact, b_sig, b_bada, b_ngs, b_modp, b_modg = (S.buf(n) for n in ("cc", "cact", "sig", "bada", "ngs", "modp", "modg"))
            b_pm = S.bufs("pm", nl)
            b_wa = S.bufs("wa", NWA)
            S.dma("sp", cc[:], ccol_all, writes=[b_cc])
            S.dma("sp", oh[:], onehot, writes=[b_cc], sem=b_cc)
            for li, l in enumerate(self.layers):
                S.dma("sp", bada[:, li, :], b_ada_c[l], writes=[b_bada], sem=b_bada)
                S.dma("sp", ngs[:, li, :].rearrange("p (k c) -> p k c", k=3),
                      ng_c[l].rearrange("k p c -> p k c"), writes=[b_ngs], sem=b_ngs)
            S.op("act", lambda e: e.activation(out=sig[:], in_=cc[:], func=AF.Sigmoid), reads=[b_cc], writes=[b_sig])
            S.op("dve", lambda e: e.tensor_tensor(out=cact[:], in0=cc[:], in1=sig[:], op=ALU.mult),
                 reads=[b_cc, b_sig], writes=[b_cact])
            loads = []
            for li, l in enumerate(self.layers):
                for g in range(ngrp):
                    i = li * ngrp + g
                    loads.append((b_wa[i % NWA], wa[i % NWA][:], w_ada_r[l][:, :, g * JG * 128:(g + 1) * JG * 128], "wa", NWA))
            strm = Stream(S, "pool", loads)
            for li, l in enumerate(self.layers):
                for g in range(ngrp):
                    i = li * ngrp + g
                    strm.ensure(i + 3)
                    w = wa[i % NWA]
                    mms = []
                    for jj in range(JG):
                        j = g * JG + jj
                        for kc in range(16):
                            mms.append((pm[li][:, j * 4:(j + 1) * 4], w[:, kc, jj * 128:(jj + 1) * 128], cact[:, kc, :], kc == 0, kc == 15))
                    S.mm(mms, reads=[b_wa[i % NWA], b_cact], writes=[b_pm[li]])
                    strm.consumed("wa")
                S.op("dve", lambda e: e.tensor_copy(out=modp[:, li, :], in_=pm[li][:, 0:NJc * 4]), reads=[b_pm[li]], writes=[b_modp])
            S.dma("sp", ada_in.rearrange("(l p) c -> p l c", p=128), modp[:], reads=[b_modp], sem=b_modp)
            S.barrier()
            if "cc" not in S.sems:
                S._newsem("cc")
            for (grp, src_, dst_) in stages:
                ins = nc.gpsimd.collective_compute("AllGather", ALU.bypass, replica_groups=grp, ins=[src_], outs=[dst_])
                S.cur["cc"] += 1
                ins.then_inc(S.sems["cc"], 1)
                S._wait("pool", [("cc", S.cur["cc"])])
            S.barrier()
            agv = ada_g.rearrange("(r l p) c -> p l r c", p=128, l=nl)
            for li in range(nl):
                S.dma("sp", modg[:, li].rearrange("p r j b -> p r (j b)"), agv[:, li], writes=[b_modg], sem=b_modg)
            for li, l in enumerate(self.layers):
                mview = self.mc[:, li, :].rearrange("p (r j) -> p r j", r=ncores)
                for bb in range(4):
                    src = modg[:, li, :, :, bb]
                    if bb == 0:
                        S.op("dve", lambda e: e.tensor_scalar(out=mview, in0=src, scalar1=oh[:, 0:1], scalar2=None, op0=ALU.mult),
                             reads=[b_modg, b_cc], writes=[self.b_mod])
                    else:
                        S.op("dve", lambda e: e.scalar_tensor_tensor(out=mview, in0=src, scalar=oh[:, bb:bb + 1], in1=mview,
                                                                     op0=ALU.mult, op1=ALU.add),
                             reads=[b_modg, b_cc], writes=[self.b_mod])
                S.op("dve", lambda e: e.tensor_tensor(out=self.mc[:, li, :], in0=self.mc[:, li, :], in1=bada[:, li, :], op=ALU.add),
                     reads=[b_bada], writes=[self.b_mod])
                for k in range(3):
                    sc = self.mc[:, li, (3 * k + 1) * 16:(3 * k + 2) * 16]
                    gt = self.mc[:, li, (3 * k + 2) * 16:(3 * k + 3) * 16]
                    S.op("dve", lambda e: e.scalar_tensor_tensor(out=self.gs[:, li, k * 16:(k + 1) * 16], in0=sc, scalar=1.0,
                                                                 in1=ngs[:, li, k * 16:(k + 1) * 16], op0=ALU.add, op1=ALU.mult),
                         reads=[b_ngs, self.b_mod], writes=[self.b_mod])
                    S.op("dve", lambda e: e.tensor_scalar(out=self.gh[:, li, k * 16:(k + 1) * 16], in0=gt,
                                                          scalar1=(1.0 if k == 1 else 0.5), scalar2=None, op0=ALU.mult),
                         reads=[self.b_mod], writes=[self.b_mod])
            S.barrier()

    def rms_stats(self, xblk, b_x, sq, b_sq, pst, b_pst, rstd, b_rstd, nfeat_chunks, tb, nfeat):
        S = self.S
        for c in range(nfeat_chunks):
            s = c % len(sq)
            S.op("act", lambda e: e.activation(out=sq[s][:, :tb], in_=xblk[:, c, :], func=AF.Square),
                 reads=[b_x], writes=[b_sq[s]])
            S.mm([(pst[:, :tb], self.ones_bf[:], sq[s][:, :tb], c == 0, c == nfeat_chunks - 1)],
                 reads=[b_sq[s], self.b_ones], writes=[b_pst])
        S.op("act", lambda e: e.activation(out=rstd[:, :tb], in_=pst[:, :tb], func=AF.Sqrt, scale=1.0 / nfeat, bias=self.eps_col[:]),
             reads=[b_pst], writes=[b_rstd])
        S.op("dve", lambda e: e.reciprocal(out=rstd[:, :tb], in_=rstd[:, :tb]), reads=[b_rstd], writes=[b_rstd])

    def ffn(self, li, which, xsrc, xdst, wgu_r, wd_r, final=None, dbg=None):
        S, nc = self.S, self.nc
        k = 0 if which == 0 else 2
        TBF = 1024
        nblk = self.T // TBF
        NJ = NFF // 2
        with contextlib.ExitStack() as st:
            hT = self.sb(st, "hT", [128, NCH, TBF], BF16)
            actT = self.sb(st, "actT", [128, NJ, TBF], BF16)
            NG, ND = 3, 3
            wgu = [self.sb(st, f"wgu{i}", [128, NCH, 256], BF16) for i in range(NG)]
            wd = [self.sb(st, f"wd{i}", [128, NJ, 128], BF16) for i in range(ND)]
            xr = [self.sb(st, f"xr{i}", [128, TBF], F32) for i in range(3)]
            xp = [self.sb(st, f"xp{i}", [128, TBF], F32) for i in range(2)]
            sq = [self.sb(st, f"sq{i}", [128, TBF], BF16) for i in range(2)]
            tmp = [self.sb(st, f"tmp{i}", [128, TBF], F32) for i in range(2)]
            sg = [self.sb(st, f"sg{i}", [128, TB], F32) for i in range(2)]
            rstd = self.sb(st, "rstd", [128, TBF], F32)
            pst = [self.ps(st, f"pst{i}", [128, 512]) for i in range(2)]
            pg = [self.ps(st, f"pg{i}", [128, 512]) for i in range(2)]
            pu = [self.ps(st, f"pu{i}", [128, 512]) for i in range(2)]
            po = [self.ps(st, f"po{i}", [128, 512]) for i in range(2)]
            b_h, b_rstd, b_pst = S.buf("h"), S.buf("rstd"), S.bufs("pst", 2)
            b_act = S.bufs("act", NJ)
            b_wgu, b_wd = S.bufs("wgu", NG), S.bufs("wd", ND)
            b_xr, b_xp = S.bufs("xr", 3), S.bufs("xp", 2)
            b_sq, b_tmp, b_sg = S.bufs("sq", 2), S.bufs("tmp", 2), S.bufs("sg", 2)
            b_pg, b_pu, b_po = S.bufs("pg", 2), S.bufs("pu", 2), S.bufs("po", 2)
            b_dr = [[S.buf(f"dr{b}_{m}") for m in range(NCH)] for b in range(nblk)]
            loads = []
            ngu = nwd = 0
            for b in range(nblk):
                for half in range(2):
                    for jj in range(NJ):
                        loads.append((b_wgu[ngu % NG], wgu[ngu % NG][:], wgu_r[half * NJ + jj], "gu", NG))
                        ngu += 1
                    for m in range(NCH):
                        loads.append((b_wd[nwd % ND], wd[nwd % ND][:], wd_r[m, half], "wd", ND))
                        nwd += 1
            strm = Stream(S, "pool", loads)
            gsl = self.gs[:, li, k * 16:(k + 1) * 16]
            shl = self.mc[:, li, (3 * k) * 16:(3 * k + 1) * 16]
            ghl = self.gh[:, li, k * 16:(k + 1) * 16]
            npc = [0]

            def pro_load(b, c):
                s = npc[0] % 2
                npc[0] += 1
                S.dma("sp", xp[s][:], xsrc[c * 128:(c + 1) * 128, b * TBF:(b + 1) * TBF], writes=[b_xp[s]])
                return s

            def pro_a(b, c, s):
                S.op("act", lambda e: e.activation(out=sq[s][:], in_=xp[s][:], func=AF.Square), reads=[b_xp[s]], writes=[b_sq[s]])
                for th in range(2):
                    S.mm([(pst[th][:], self.ones_bf[:], sq[s][:, th * 512:(th + 1) * 512], c == 0, c == NCH - 1)],
                         reads=[b_sq[s], self.b_ones], writes=[b_pst[th]])

            def pro_r(b):
                for th in range(2):
                    S.op("act", lambda e: e.activation(out=rstd[:, th * 512:(th + 1) * 512], in_=pst[th][:], func=AF.Sqrt, scale=1.0 / D,
                                                       bias=self.eps_col[:]), reads=[b_pst[th]], writes=[b_rstd])
                S.op("dve", lambda e: e.reciprocal(out=rstd[:], in_=rstd[:]), reads=[b_rstd], writes=[b_rstd])

            def pro_b(b, c, s):
                S.op("dve", lambda e: e.scalar_tensor_tensor(out=tmp[s][:], in0=xp[s][:], scalar=gsl[:, c:c + 1], in1=rstd[:],
                                                             op0=ALU.mult, op1=ALU.mult),
                     reads=[b_xp[s], b_rstd, self.b_mod], writes=[b_tmp[s]])
                S.op("act", lambda e: e.activation(out=hT[:, c, :], in_=tmp[s][:], func=AF.Identity, bias=shl[:, c:c + 1], scale=1.0),
                     reads=[b_tmp[s], self.b_mod], writes=[b_h])

            nres = [0]
            res_slot = {}

            def ld_res(b, half, m):
                x3 = nres[0] % 3
                nres[0] += 1
                res_slot[(b, half, m)] = x3
                src = xsrc if half == 0 else xdst
                S.dma("sp", xr[x3][:], src[m * 128:(m + 1) * 128, b * TBF:(b + 1) * TBF],
                      reads=[b_dr[b][m]] if half == 1 else [], writes=[b_xr[x3]])

            strm.ensure(2)
            for c in range(NCH):
                pro_a(0, c, pro_load(0, c))
            pro_r(0)
            for c in range(NCH):
                pro_b(0, c, pro_load(0, c))
            step = 0
            npsum = 0
            for b in range(nblk):
                for half in range(2):
                    for jj in range(NJ):
                        strm.ensure(step + 2)
                        w = wgu[(step_gu := (b * 2 + half) * NJ + jj) % NG]
                        bw = b_wgu[step_gu % NG]
                        early = (half == 1 and b + 1 < nblk)
                        pcs_a = []
                        if early and jj < 8:
                            pcs_a = [(c, pro_load(b + 1, c)) for c in (2 * jj, 2 * jj + 1)]
                        for th in range(2):
                            s = npsum % 2
                            npsum += 1
                            tsl = slice(th * 512, (th + 1) * 512)
                            S.mm([(pg[s][:], w[:, c, 0:128], hT[:, c, tsl], c == 0, c == NCH - 1) for c in range(NCH)],
                                 reads=[bw, b_h], writes=[b_pg[s]])
                            S.mm([(pu[s][:], w[:, c, 128:256], hT[:, c, tsl], c == 0, c == NCH - 1) for c in range(NCH)],
                                 reads=[bw, b_h], writes=[b_pu[s]])
                            S.op("act", lambda e: e.activation(out=sg[s][:], in_=pg[s][:], func=AF.Silu),
                                 reads=[b_pg[s]], writes=[b_sg[s]])
                            S.op("dve", lambda e: e.tensor_tensor(out=actT[:, jj, tsl], in0=sg[s][:], in1=pu[s][:], op=ALU.mult),
                                 reads=[b_sg[s], b_pu[s]], writes=[b_act[jj]])
                        strm.consumed("gu")
                        step += 1
                        if early:
                            for c, ps_ in pcs_a:
                                pro_a(b + 1, c, ps_)
                            if jj == 8:
                                pro_r(b + 1)
                    ld_res(b, half, 0)
                    for m in range(NCH):
                        strm.ensure(step + 2)
                        n_wd = (b * 2 + half) * NCH + m
                        w = wd[n_wd % ND]
                        if m + 1 < NCH:
                            ld_res(b, half, m + 1)
                        overlap = (half == 1 and b + 1 < nblk)
                        pcs = []
                        if overlap and m < 8:
                            for c in (2 * m, 2 * m + 1):
                                pcs.append((c, pro_load(b + 1, c)))
                        for th in range(2):
                            tsl = slice(th * 512, (th + 1) * 512)
                            S.mm([(po[th][:], w[:, jj, :], actT[:, jj, tsl], jj == 0, jj == NJ - 1) for jj in range(NJ)],
                                 reads=[b_wd[n_wd % ND]] + b_act, writes=[b_po[th]])
                        strm.consumed("wd")
                        if overlap:
                            for c, ps_ in pcs:
                                pro_b(b + 1, c, ps_)
                        x3 = res_slot[(b, half, m)]
                        for th in range(2):
                            tsl = slice(th * 512, (th + 1) * 512)
                            S.op("dve", lambda e: e.scalar_tensor_tensor(out=xr[x3][:, tsl], in0=po[th][:], scalar=ghl[:, m:m + 1],
                                                                         in1=xr[x3][:, tsl], op0=ALU.mult, op1=ALU.add),
                                 reads=[b_po[th], self.b_mod], writes=[b_xr[x3]])
                        S.dma("sp", xdst[m * 128:(m + 1) * 128, b * TBF:(b + 1) * TBF], xr[x3][:], reads=[b_xr[x3]],
                              writes=[b_dr[b][m]], sem=b_xr[x3])
                        step += 1
            S.barrier()
        if final is not None:
            self.final_norm(xdst, final[0], final[1])

    def final_norm(self, xsrc, fcol, outd):
        S = self.S
        with contextlib.ExitStack() as st:
            xk = [self.sb(st, f"xk{i}", [128, NCH, TB], F32) for i in range(2)]
            sq = [self.sb(st, f"fsq{i}", [128, TB], BF16) for i in range(2)]
            oc = [self.sb(st, f"foc{i}", [128, TB], F32) for i in range(4)]
            rstd = [self.sb(st, f"frstd{i}", [128, TB], F32) for i in range(2)]
            pst = [self.ps(st, f"pst{i}", [128, 512]) for i in range(2)]
            b_xk = [S.bufs(f"xk{i}_", NCH) for i in range(2)]
            b_sq, b_oc = S.bufs("fsq", 2), S.bufs("foc", 4)
            b_rstd, b_pst = S.bufs("frstd", 2), S.bufs("pst", 2)
            n = 0

            def ld_blk(b):
                if b >= self.nblk:
                    return
                for c in range(NCH):
                    S.dma("sp", xk[b % 2][:, c, :], xsrc[c * 128:(c + 1) * 128, b * TB:(b + 1) * TB], writes=[b_xk[b % 2][c]])
            ld_blk(0)
            for b in range(self.nblk):
                tsl = slice(b * TB, (b + 1) * TB)
                k2 = b % 2
                ld_blk(b + 1)
                for c in range(NCH):
                    s = c % 2
                    S.op("act", lambda e: e.activation(out=sq[s][:], in_=xk[k2][:, c, :], func=AF.Square), reads=[b_xk[k2][c]], writes=[b_sq[s]])
                    S.mm([(pst[k2][:], self.ones_bf[:], sq[s][:], c == 0, c == NCH - 1)], reads=[b_sq[s], self.b_ones], writes=[b_pst[k2]])
                S.op("act", lambda e: e.activation(out=rstd[k2][:], in_=pst[k2][:], func=AF.Sqrt, scale=1.0 / D, bias=self.eps_col[:]),
                     reads=[b_pst[k2]], writes=[b_rstd[k2]])
                S.op("dve", lambda e: e.reciprocal(out=rstd[k2][:], in_=rstd[k2][:]), reads=[b_rstd[k2]], writes=[b_rstd[k2]])
                for c in range(NCH):
                    o3 = n % 4
                    n += 1
                    S.op("dve", lambda e: e.scalar_tensor_tensor(out=oc[o3][:], in0=xk[k2][:, c, :], scalar=fcol[:, c:c + 1], in1=rstd[k2][:],
                                                                 op0=ALU.mult, op1=ALU.mult),
                         reads=[b_xk[k2][c], b_rstd[k2], self.b_cst], writes=[b_oc[o3]])
                    S.dma("pool", outd[c * 128:(c + 1) * 128, tsl], oc[o3][:], reads=[b_oc[o3]], sem=b_oc[o3])
            S.barrier()

C_INV, C_SIGN, C_NEGPI, C_CD = 0, 1, 2, 3
C_MASK = 8
C_KDEC = C_MASK + 1024
C_QDEC = C_KDEC + 512
C_ID = C_QDEC + 512
NCONST = C_ID + 128
PI = float(np.pi)
NSLAB = 16
RET_H = 8
SM_SCALE = float(192 ** -0.5)


def host_consts():
    c = np.zeros((128, NCONST), np.float32)
    p = np.arange(128)
    inv = (10000.0 ** (-np.arange(0, 64, 2, dtype=np.float32) / 64)).astype(np.float32)
    c[:, C_INV] = inv[p % 32]
    c[:, C_SIGN] = np.where((p % 64) < 32, -1.0, 1.0)
    c[:, C_NEGPI] = -np.pi
    log_g = np.log1p(-(2.0 ** (-5.0 - np.arange(8, dtype=np.float32)))).astype(np.float32)
    idx = np.arange(128, dtype=np.float32)
    for hp in range(4):
        for half in range(2):
            h = 2 * hp + half
            rows = slice(half * 64, half * 64 + 64)
            c[rows, C_CD + hp] = np.exp(log_g[h] * 128.0)
            c[rows, C_QDEC + hp * 128: C_QDEC + (hp + 1) * 128] = np.exp(log_g[h] * (idx + 1.0))[None, :]
            rel = idx[None, :] - idx[:, None]
            m = np.where(rel >= 0, np.exp(log_g[h] * np.maximum(rel, 0.0)), 0.0) * 0.125
            c[:, C_MASK + hp * 256 + half * 128: C_MASK + hp * 256 + (half + 1) * 128] = m
    for h in range(8):
        c[:, C_KDEC + h * 64: C_KDEC + (h + 1) * 64] = (0.125 * np.exp(log_g[h] * (127.0 - idx)))[:, None]
    c[:, C_ID: C_ID + 128] = np.eye(128, dtype=np.float32)
    return c


def host_cmask():
    p = np.arange(128)
    i = np.arange(512)
    c = np.zeros((128, 2048), np.float32)
    for r in range(4):
        c[:, r * 512:(r + 1) * 512] = ((r * 128 + p)[:, None] <= i[None, :]).astype(np.float32)
    return c


def mix_consts(P, cst, flag, sm_c):
    S, st = P.S, P.stack
    nl = len(P.layers)
    P.cst = P.sb(st, "cst", [128, NCONST], F32)
    P.flag = P.sb(st, "flag", [128, 1], F32)
    P.smc = P.sb(st, "smc", [128, nl, 16], F32)
    P.ident = P.sb(st, "ident", [128, 128], BF16)
    P.flagmat = P.sb(st, "flagmat", [128, 128], BF16)
    P.b_cst = S.buf("cst")
    S.dma("sp", P.cst[:], cst, writes=[P.b_cst])
    S.dma("sp", P.flag[:], flag, writes=[P.b_cst], sem=P.b_cst)
    for li in range(nl):
        S.dma("sp", P.smc[:, li, :], sm_c[li], writes=[P.b_cst], sem=P.b_cst)
    S.op("dve", lambda e: e.tensor_copy(out=P.ident[:], in_=P.cst[:, C_ID:C_ID + 128]), reads=[P.b_cst], writes=[P.b_ones])
    S.op("dve", lambda e: e.tensor_scalar(out=P.flagmat[:], in0=P.cst[:, C_ID:C_ID + 128], scalar1=0.0, scalar2=P.flag[:, 0:1],
                                          op0=ALU.mult, op1=ALU.add), reads=[P.b_cst], writes=[P.b_ones])


def rope_tables(P, pos):
    S, st, T = P.S, P.stack, P.T
    P.cos2 = P.sb(st, "cos2", [128, T], F32)
    P.sin2 = P.sb(st, "sin2", [128, T], F32)
    P.b_rope = S.buf("rope")
    C1 = 6.28125
    C2 = 2 * np.pi - 6.28125
    with contextlib.ExitStack() as s2:
        posi = P.sb(s2, "posi", [128, T], I32)
        ang = P.sb(s2, "ang", [128, T], F32)
        r = P.sb(s2, "rr_", [128, T], F32)
        m = P.sb(s2, "mm_", [128, T], F32)
        b_pos, b_ang, b_r, b_m = S.buf("posi"), S.buf("ang"), S.buf("r"), S.buf("m")
        dv = lambda fn, rd, wr: S.op("dve", fn, reads=rd, writes=wr)
        S.dma("sp", posi[:], pos.partition_broadcast(128), writes=[b_pos])
        dv(lambda e: e.tensor_copy(out=ang[:], in_=posi[:]), [b_pos], [b_ang])
        dv(lambda e: e.tensor_scalar(out=ang[:], in0=ang[:], scalar1=P.cst[:, C_INV:C_INV + 1], scalar2=None, op0=ALU.mult),
           [P.b_cst], [b_ang])
        dv(lambda e: e.tensor_scalar(out=m[:], in0=ang[:], scalar1=float(1.0 / (2 * np.pi)), scalar2=None, op0=ALU.mult), [b_ang], [b_m])
        dv(lambda e: e.tensor_copy(out=posi[:], in_=m[:]), [b_m], [b_pos])
        dv(lambda e: e.tensor_copy(out=m[:], in_=posi[:]), [b_pos], [b_m])
        dv(lambda e: e.scalar_tensor_tensor(out=r[:], in0=m[:], scalar=-C1, in1=ang[:], op0=ALU.mult, op1=ALU.add), [b_m, b_ang], [b_r])
        dv(lambda e: e.scalar_tensor_tensor(out=r[:], in0=m[:], scalar=-C2, in1=r[:], op0=ALU.mult, op1=ALU.add), [b_m], [b_r])

        def wrap_sin(dst, after=None):
            dv(lambda e: e.tensor_scalar(out=m[:], in0=r[:], scalar1=PI, scalar2=-2 * PI, op0=ALU.is_gt, op1=ALU.mult), [b_r], [b_m])
            dv(lambda e: e.tensor_tensor(out=r[:], in0=r[:], in1=m[:], op=ALU.add), [b_m], [b_r])
            dv(lambda e: e.tensor_scalar(out=m[:], in0=r[:], scalar1=-PI, scalar2=PI, op0=ALU.max, op1=ALU.min), [b_r], [b_m])
            S.op("act", lambda e: e.activation(out=dst[:], in_=m[:], func=AF.Sin), reads=[b_m], writes=[P.b_rope])

        wrap_sin(P.sin2)
        dv(lambda e: e.tensor_scalar(out=P.sin2[:], in0=P.sin2[:], scalar1=P.cst[:, C_SIGN:C_SIGN + 1], scalar2=None, op0=ALU.mult),
           [P.b_cst], [P.b_rope])
        dv(lambda e: e.tensor_scalar(out=r[:], in0=r[:], scalar1=PI / 2, scalar2=None, op0=ALU.add), [], [b_r])
        wrap_sin(P.cos2)
        S.barrier()


def norm_block(P, li, k, xsrc, t0, hT, b_h, xc, b_xc, sq, b_sq, tmp, b_tmp, pst, b_pst, rstd, b_rstd):
    S = P.S
    gsl = P.gs[:, li, k * 16:(k + 1) * 16]
    shl = P.mc[:, li, (3 * k) * 16:(3 * k + 1) * 16]
    for c in range(NCH):
        s = c % 2
        S.dma("sp", xc[s][:], xsrc[c * 128:(c + 1) * 128, t0:t0 + TB], writes=[b_xc[s]])
        S.op("act", lambda e: e.activation(out=sq[s][:], in_=xc[s][:], func=AF.Square), reads=[b_xc[s]], writes=[b_sq[s]])
        S.mm([(pst[:], P.ones_bf[:], sq[s][:], c == 0, c == NCH - 1)], reads=[b_sq[s], P.b_ones], writes=[b_pst])
    S.op("act", lambda e: e.activation(out=rstd[:], in_=pst[:], func=AF.Sqrt, scale=1.0 / D, bias=P.eps_col[:]),
         reads=[b_pst], writes=[b_rstd])
    S.op("dve", lambda e: e.reciprocal(out=rstd[:], in_=rstd[:]), reads=[b_rstd], writes=[b_rstd])
    for c in range(NCH):
        s = c % 2
        S.dma("sp", xc[s][:], xsrc[c * 128:(c + 1) * 128, t0:t0 + TB], writes=[b_xc[s]])
        S.op("dve", lambda e: e.scalar_tensor_tensor(out=tmp[s][:], in0=xc[s][:], scalar=gsl[:, c:c + 1], in1=rstd[:],
                                                     op0=ALU.mult, op1=ALU.mult),
             reads=[b_xc[s], b_rstd, P.b_mod], writes=[b_tmp[s]])
        S.op("act", lambda e: e.activation(out=hT[:, c, :], in_=tmp[s][:], func=AF.Identity, bias=shl[:, c:c + 1], scale=1.0),
             reads=[b_tmp[s], P.b_mod], writes=[b_h])


def mixer(P, li, xs, w_inF, w_inV, w_uq_r, w_ukv_r, wo_r, D_, group):
    S, nc, T = P.S, P.nc, P.T
    NT = T // 128
    nblk = P.nblk
    cst = P.cst
    with contextlib.ExitStack() as mst:
        qT = P.sb(mst, "qT", [128, 4, T], BF16)
        kT = P.sb(mst, "kT", [128, 4, T], BF16)
        kd_tm = P.sb(mst, "kd_tm", [128, NT, 512], BF16)
        v_tm = P.sb(mst, "v_tm", [128, NT, 1024], BF16)
        b_qT, b_kT = S.bufs("qT", nblk), S.bufs("kT", nblk)
        b_kd, b_v = S.bufs("kd", NT), S.bufs("v", NT)
        with contextlib.ExitStack() as st:
            hTs = [P.sb(st, f"hT{i}", [128, NCH, TB], BF16) for i in range(2)]
            xc = [P.sb(st, f"xc{i}", [128, TB], F32) for i in range(2)]
            sq = [P.sb(st, f"sq{i}", [128, TB], BF16) for i in range(2)]
            tmp = [P.sb(st, f"tmp{i}", [128, TB], F32) for i in range(3)]
            rstd = P.sb(st, "rstd", [128, TB], F32)
            rstdx = [P.sb(st, f"rstdx{i}", [128, TB], F32) for i in range(2)]
            wF = [P.sb(st, f"wF{i}", [128, NCH, 128], BF16) for i in range(4)]
            wV = [P.sb(st, f"wV{i}", [128, NCH, 256], BF16) for i in range(2)]
            cq32 = P.sb(st, "cq32", [128, 4, TB], F32)
            stg = [P.sb(st, f"stg{i}", [128, 4, TB], BF16) for i in range(2)]
            pst = P.ps(st, "pst", [128, 512])
            pa = [P.ps(st, f"pa{i}", [128, 512]) for i in range(2)]
            pb = [P.ps(st, f"pb{i}", [128, 512]) for i in range(2)]
            pv = [P.ps(st, f"pv{i}", [128, 512]) for i in range(2)]
            pt = P.ps(st, "pt", [128, 1024], BF16)
            b_hs, b_rstd, b_pst, b_cq32, b_pt = S.bufs("h", 2), S.buf("rstd"), S.buf("pst"), S.buf("cq32"), S.buf("pt")
            b_rstdx = S.bufs("rstdx", 2)
            b_xc, b_sq, b_tmp = S.bufs("xc", 2), S.bufs("sq", 2), S.bufs("tmp", 3)
            gsl_ = P.gs[:, li, 16:32]
            shl_ = P.mc[:, li, 3 * 16:4 * 16]
            npc = [0]

            def pro_load(b, c):
                s_ = npc[0] % 2
                npc[0] += 1
                S.dma("sp", xc[s_][:], xs[c * 128:(c + 1) * 128, b * TB:(b + 1) * TB], writes=[b_xc[s_]])
                return s_

            def pro_a(b, c, s_):
                S.op("act", lambda e: e.activation(out=sq[s_][:], in_=xc[s_][:], func=AF.Square), reads=[b_xc[s_]], writes=[b_sq[s_]])
                S.mm([(pst[:], P.ones_bf[:], sq[s_][:], c == 0, c == NCH - 1)], reads=[b_sq[s_], P.b_ones], writes=[b_pst])

            def pro_r(b):
                nb = b % 2
                S.op("act", lambda e: e.activation(out=rstdx[nb][:], in_=pst[:], func=AF.Sqrt, scale=1.0 / D, bias=P.eps_col[:]),
                     reads=[b_pst], writes=[b_rstdx[nb]])
                S.op("dve", lambda e: e.reciprocal(out=rstdx[nb][:], in_=rstdx[nb][:]), reads=[b_rstdx[nb]], writes=[b_rstdx[nb]])

            def pro_b(b, c, s_):
                nb = b % 2
                S.op("dve", lambda e: e.scalar_tensor_tensor(out=tmp[s_][:], in0=xc[s_][:], scalar=gsl_[:, c:c + 1], in1=rstdx[nb][:],
                                                             op0=ALU.mult, op1=ALU.mult),
                     reads=[b_xc[s_], b_rstdx[nb], P.b_mod], writes=[b_tmp[s_]])
                S.op("act", lambda e: e.activation(out=hTs[nb][:, c, :], in_=tmp[s_][:], func=AF.Identity, bias=shl_[:, c:c + 1], scale=1.0),
                     reads=[b_tmp[s_], P.b_mod], writes=[b_hs[nb]])
            b_wF, b_wV, b_stg = S.bufs("wF", 4), S.bufs("wV", 2), S.bufs("stg", 2)
            b_pa, b_pb, b_pv = S.bufs("pa", 2), S.bufs("pb", 2), S.bufs("pv", 2)
            loads = []
            for b in range(nblk):
                for sl in range(NSLAB):
                    for ch in range(2):
                        n = (b * NSLAB + sl) * 2 + ch
                        loads.append((b_wF[n % 4], wF[n % 4][:], w_inF[sl][:, ch], "wF", 4))
                for vs in range(4):
                    n = b * 4 + vs
                    loads.append((b_wV[n % 2], wV[n % 2][:], w_inV[vs], "wV", 2))
            strm = Stream(S, "pool", loads)
            step = 0
            nstg = 0
            strm.ensure(3)
            for c in range(NCH):
                pro_a(0, c, pro_load(0, c))
            pro_r(0)
            for c in range(NCH):
                pro_b(0, c, pro_load(0, c))
            for b in range(nblk):
                t0 = b * TB
                tsl = slice(t0, t0 + TB)
                hT, b_h = hTs[b % 2], b_hs[b % 2]
                strm.ensure(step + 3)
                for sl in range(NSLAB):
                    strm.ensure(step + 3)
                    n0 = (b * NSLAB + sl) * 2
                    s = sl % 2
                    pcs = []
                    if b + 1 < nblk:
                        cc_ = (2 * sl, 2 * sl + 1) if sl < 8 else (2 * (sl - 8), 2 * (sl - 8) + 1)
                        pcs = [(c, pro_load(b + 1, c)) for c in cc_]
                    S.mm([(pa[s][:], wF[n0 % 4][:, c, :], hT[:, c, :], c == 0, c == NCH - 1) for c in range(NCH)],
                         reads=[b_wF[n0 % 4], b_h], writes=[b_pa[s]])
                    strm.consumed("wF")
                    step += 1
                    strm.ensure(step + 3)
                    S.mm([(pb[s][:], wF[(n0 + 1) % 4][:, c, :], hT[:, c, :], c == 0, c == NCH - 1) for c in range(NCH)],
                         reads=[b_wF[(n0 + 1) % 4], b_h], writes=[b_pb[s]])
                    strm.consumed("wF")
                    step += 1
                    if b + 1 < nblk:
                        if sl == 8:
                            pro_r(b + 1)
                        for c, ps_ in pcs:
                            (pro_a if sl < 8 else pro_b)(b + 1, c, ps_)
                    if sl < 8:
                        hp = sl % 4
                        dst, bd = (qT, b_qT) if sl < 4 else (kT, b_kT)
                        S.op("dve", lambda e: e.tensor_tensor(out=tmp[0][:], in0=pa[s][:], in1=P.cos2[:, tsl], op=ALU.mult),
                             reads=[b_pa[s], P.b_rope], writes=[b_tmp[0]])
                        S.op("dve", lambda e: e.tensor_tensor(out=tmp[1][:], in0=pb[s][:], in1=P.sin2[:, tsl], op=ALU.mult),
                             reads=[b_pb[s], P.b_rope], writes=[b_tmp[1]])
                        S.op("dve", lambda e: e.tensor_tensor(out=dst[:, hp, tsl], in0=tmp[0][:], in1=tmp[1][:], op=ALU.add),
                             reads=[b_tmp[0], b_tmp[1]], writes=[bd[b]])
                    elif sl < 12:
                        for which, pp, bp in ((0, pa[s], b_pa[s]), (1, pb[s], b_pb[s])):
                            h = (sl - 8) * 2 + which
                            sb_ = stg[nstg % 2]
                            S.op("act", lambda e: e.activation(out=tmp[2][:], in_=pp[:], func=AF.Silu), reads=[bp], writes=[b_tmp[2]])
                            S.op("dve", lambda e: e.tensor_scalar(out=sb_[:, 0, :], in0=tmp[2][:], scalar1=P.smc[:, li, h:h + 1],
                                                                  scalar2=None, op0=ALU.mult),
                                 reads=[b_tmp[2], P.b_cst], writes=[b_stg[nstg % 2]])
                            S.dma("sp", D_["gsd"][h * 128:(h + 1) * 128, tsl], sb_[:, 0, :], reads=[b_stg[nstg % 2]], sem=b_stg[nstg % 2])
                            nstg += 1
                    elif sl < 15:
                        base = (sl - 12) * 2 if sl < 14 else 0
                        nchk = 4 if sl < 14 else 2
                        for which, pp, bp in ((0, pa[s], b_pa[s]), (1, pb[s], b_pb[s])):
                            ci = base + which
                            first = (ci == 0)
                            last = (ci == nchk - 1)
                            S.op("dve", lambda e: e.tensor_copy(out=cq32[:, ci, :], in_=pp[:]), reads=[bp], writes=[b_cq32])
                            S.op("act", lambda e: e.activation(out=sq[which][:], in_=pp[:], func=AF.Square), reads=[bp], writes=[b_sq[which]])
                            S.mm([(pst[:], P.ones_bf[:], sq[which][:], first, last)], reads=[b_sq[which], P.b_ones], writes=[b_pst])
                        if sl == 13 or sl == 14:
                            S.op("act", lambda e: e.activation(out=rstd[:], in_=pst[:], func=AF.Sqrt, scale=1.0 / (128 * nchk), bias=P.eps_col[:]),
                                 reads=[b_pst], writes=[b_rstd])
                            S.op("dve", lambda e: e.reciprocal(out=rstd[:], in_=rstd[:]), reads=[b_rstd], writes=[b_rstd])
                            sb_ = stg[nstg % 2]
                            for ci in range(nchk):
                                S.op("dve", lambda e: e.tensor_tensor(out=sb_[:, ci, :], in0=cq32[:, ci, :], in1=rstd[:], op=ALU.mult),
                                     reads=[b_cq32, b_rstd], writes=[b_stg[nstg % 2]])
                            dd = D_["cqs"] if sl == 13 else D_["xkv"]
                            S.dma("sp", dd[:, tsl].rearrange("(c p) t -> p c t", p=128), sb_[:, 0:nchk, :],
                                  reads=[b_stg[nstg % 2]], sem=b_stg[nstg % 2])
                            nstg += 1
                    else:
                        sb_ = stg[nstg % 2]
                        S.op("dve", lambda e: e.tensor_tensor(out=tmp[0][:], in0=pa[s][:], in1=P.cos2[:, tsl], op=ALU.mult),
                             reads=[b_pa[s], P.b_rope], writes=[b_tmp[0]])
                        S.op("dve", lambda e: e.tensor_tensor(out=tmp[1][:], in0=pb[s][:], in1=P.sin2[:, tsl], op=ALU.mult),
                             reads=[b_pb[s], P.b_rope], writes=[b_tmp[1]])
                        S.op("dve", lambda e: e.tensor_tensor(out=sb_[:, 0, :], in0=tmp[0][:], in1=tmp[1][:], op=ALU.add),
                             reads=[b_tmp[0], b_tmp[1]], writes=[b_stg[nstg % 2]])
                        S.dma("sp", D_["xpe"][:, tsl], sb_[0:64, 0, :], reads=[b_stg[nstg % 2]], sem=b_stg[nstg % 2])
                        nstg += 1
                for tt in range(4):
                    tile_i = b * 4 + tt
                    half = tt % 2
                    S.pe_ops([(lambda e, hp=hp: e.transpose(pt[:, half * 512 + hp * 128: half * 512 + (hp + 1) * 128],
                                                           kT[:, hp, t0 + tt * 128: t0 + (tt + 1) * 128], P.ident[:]))
                              for hp in range(4)], reads=[b_kT[b], P.b_ones], writes=[b_pt])
                    S.op("dve", lambda e: e.tensor_tensor(out=kd_tm[:, tile_i, :], in0=pt[:, half * 512:(half + 1) * 512],
                                                          in1=cst[:, C_KDEC:C_KDEC + 512], op=ALU.mult),
                         reads=[b_pt, P.b_cst], writes=[b_kd[tile_i]])
                for vs in range(4):
                    strm.ensure(step + 3)
                    n = b * 4 + vs
                    w = wV[n % 2]
                    for tt in range(4):
                        tile_i = b * 4 + tt
                        s = tt % 2
                        S.mm([(pv[s][:, 0:256], hT[:, c, tt * 128:(tt + 1) * 128], w[:, c, :], c == 0, c == NCH - 1) for c in range(NCH)],
                             reads=[b_wV[n % 2], b_h], writes=[b_pv[s]])
                        S.op("act", lambda e: e.activation(out=v_tm[:, tile_i, vs * 256:(vs + 1) * 256], in_=pv[s][:, 0:256], func=AF.Copy),
                             reads=[b_pv[s]], writes=[b_v[tile_i]])
                    strm.consumed("wV")
                    step += 1
            S.barrier()
            if getattr(P, "mix_stop", None) == "P":
                return
        with contextlib.ExitStack() as st:
            NTc = NT
            state = P.sb(st, "state", [128, 4, 256], F32)
            states_bf = [P.sb(st, f"states_bf{i}", [128, NTc + 1, 256], BF16) for i in range(2)]
            stload = P.sb(st, "stload", [128, 4, 256], BF16)
            SmT = [P.sb(st, f"SmT{i}", [128, 256], BF16) for i in range(3)]
            qd = [P.sb(st, f"qd{i}", [128, 128], BF16) for i in range(3)]
            sqr = [P.sb(st, f"sqr{i}", [128, TB], BF16) for i in range(2)]
            gbuf = [P.sb(st, f"gbuf{i}", [128, TB], BF16) for i in range(2)]
            rr = [P.sb(st, f"rr{i}", [128, TB], F32) for i in range(2)]
            tt_ = [P.sb(st, f"tr{i}", [128, TB], F32) for i in range(2)]
            ystg = [P.sb(st, f"ystg{i}", [128, TB], BF16) for i in range(2)]
            pS = [[P.ps(st, f"pS{x}{i}", [128, 512]) for i in range(2)] for x in range(2)]
            po = [P.ps(st, f"po{x}", [128, 512]) for x in range(2)]
            pinc = P.ps(st, "pinc", [128, 512])
            pss = P.ps(st, "pss", [128, 512])
            b_stl, b_pinc, b_pss = S.buf("stl"), S.buf("pinc"), S.buf("pss")
            b_SmT, b_qd = S.bufs("SmT", 3), S.bufs("qd", 3)
            b_sqr, b_g, b_rr, b_tr, b_ystg = (S.bufs(n, 2) for n in ("sqr", "g", "rr", "tr", "ystg"))
            b_pS = [S.bufs(f"pS{x}", 2) for x in range(2)]
            b_po = S.bufs("po", 2)
            b_st4 = S.bufs("st4", 4)
            b_sts = [S.bufs(f"sts{i}_", NTc + 1) for i in range(2)]

            def state_init(hp, zero):
                sb = hp % 2
                if zero:
                    S.op("dve", lambda e: e.memset(state[:, hp, :], 0.0), writes=[b_st4[hp]])
                else:
                    S.op("dve", lambda e: e.tensor_scalar(out=state[:, hp, :], in0=stload[:, hp, :], scalar1=P.flag[:, 0:1],
                                                          scalar2=None, op0=ALU.mult),
                         reads=[b_stl, P.b_cst], writes=[b_st4[hp]])
                S.op("act", lambda e: e.activation(out=states_bf[sb][:, 0, :], in_=state[:, hp, :], func=AF.Copy),
                     reads=[b_st4[hp]], writes=[b_sts[sb][0]])

            def state_step(hp, n):
                sb = hp % 2
                S.mm([(pinc[:, 0:256], kd_tm[:, n, hp * 128:(hp + 1) * 128], v_tm[:, n, hp * 256:(hp + 1) * 256], True, True)],
                     reads=[b_kd[n], b_v[n]], writes=[b_pinc])
                S.op("dve", lambda e: e.scalar_tensor_tensor(out=state[:, hp, :], in0=state[:, hp, :], scalar=cst[:, C_CD + hp:C_CD + hp + 1],
                                                             in1=pinc[:, 0:256], op0=ALU.mult, op1=ALU.add),
                     reads=[b_pinc, P.b_cst], writes=[b_st4[hp]])
                S.op("act", lambda e: e.activation(out=states_bf[sb][:, n + 1, :], in_=state[:, hp, :], func=AF.Copy),
                     reads=[b_st4[hp]], writes=[b_sts[sb][n + 1]])

            xfin = P.sb(st, "xfin", [128, 4, 256], BF16)
            b_xfin = S.buf("xfin")
            for hp in range(4):
                S.op("dve", lambda e: e.memset(state[:, hp, :], 0.0), writes=[b_st4[hp]])
            pincs = [pinc, pss]
            b_pincs = [b_pinc, b_pss]
            for n in range(NTc):
                for hp in range(4):
                    pi = (n * 4 + hp) % 2
                    S.mm([(pincs[pi][:, 0:256], kd_tm[:, n, hp * 128:(hp + 1) * 128], v_tm[:, n, hp * 256:(hp + 1) * 256], True, True)],
                         reads=[b_kd[n], b_v[n]], writes=[b_pincs[pi]])
                    S.op("dve", lambda e: e.scalar_tensor_tensor(out=state[:, hp, :], in0=state[:, hp, :], scalar=cst[:, C_CD + hp:C_CD + hp + 1],
                                                                 in1=pincs[pi][:, 0:256], op0=ALU.mult, op1=ALU.add),
                         reads=[b_pincs[pi], P.b_cst], writes=[b_st4[hp]])
            for hp in range(4):
                S.op("act", lambda e: e.activation(out=xfin[:, hp, :], in_=state[:, hp, :], func=AF.Copy), reads=[b_st4[hp]], writes=[b_xfin])
            S.dma("sp", D_["xst"].rearrange("(h p) e -> p h e", p=128), xfin[:], reads=[b_xfin], sem=b_xfin)
            S.barrier()
            if getattr(P, "mix_stop", None) == "R1":
                return
            exchange(P, D_, group)
            if getattr(P, "mix_stop", None) == "X":
                return
            S.dma("sp", stload[:], D_["gst"][0:512, :].rearrange("(h p) e -> p h e", p=128), writes=[b_stl])

            state_init(0, False)
            for n in range(NTc):
                state_step(0, n)
            cnt = 0
            ny = 0
            for hp in range(4):
                sb = hp % 2
                if hp + 1 < 4:
                    state_init(hp + 1, False)

                def emit_S(n):
                    tsl = slice(n * 128, (n + 1) * 128)
                    blk = n // 4
                    for x in range(2):
                        rows = slice(x * 64, x * 64 + 64)
                        S.mm([(pS[x][n % 2][:, 0:128], kT[rows, hp, tsl], qT[rows, hp, tsl], True, True)],
                             reads=[b_kT[blk], b_qT[blk]], writes=[b_pS[x][n % 2]])

                emit_S(0)
                for n in range(NTc):
                    blk, n4 = divmod(n, 4)
                    tsl = slice(n * 128, (n + 1) * 128)
                    s = cnt % 3
                    cnt += 1
                    if n + 1 < NTc:
                        emit_S(n + 1)
                    for x in range(2):
                        S.op("dve", lambda e: e.tensor_tensor(out=SmT[s][:, x * 128:(x + 1) * 128], in0=pS[x][n % 2][:, 0:128],
                                                              in1=cst[:, C_MASK + hp * 256 + x * 128: C_MASK + hp * 256 + (x + 1) * 128], op=ALU.mult),
                             reads=[b_pS[x][n % 2], P.b_cst], writes=[b_SmT[s]])
                    S.op("dve", lambda e: e.tensor_tensor(out=qd[s][:], in0=qT[:, hp, tsl],
                                                          in1=cst[:, C_QDEC + hp * 128: C_QDEC + (hp + 1) * 128], op=ALU.mult),
                         reads=[b_qT[blk], P.b_cst], writes=[b_qd[s]])
                    for x in range(2):
                        h = 2 * hp + x
                        rows = slice(x * 64, x * 64 + 64)
                        S.mm([(po[x][:, n4 * 128:(n4 + 1) * 128], v_tm[:, n, h * 128:(h + 1) * 128], SmT[s][:, x * 128:(x + 1) * 128], True, False),
                              (po[x][:, n4 * 128:(n4 + 1) * 128], states_bf[sb][rows, n, x * 128:(x + 1) * 128], qd[s][rows, :], False, True)],
                             reads=[b_v[n], b_SmT[s], b_sts[sb][n], b_qd[s]], writes=[b_po[x]])
                    if hp + 1 < 4:
                        state_step(hp + 1, n)
                    if n4 == 3:
                        t0 = blk * TB
                        for x in range(2):
                            h = 2 * hp + x
                            s2 = ny % 2
                            ny += 1
                            S.dma("sp", gbuf[s2][:], D_["gsd"][h * 128:(h + 1) * 128, t0:t0 + TB], writes=[b_g[s2]])
                            S.op("act", lambda e: e.activation(out=sqr[s2][:], in_=po[x][:], func=AF.Square),
                                 reads=[b_po[x]], writes=[b_sqr[s2]])
                            S.op("act", lambda e: e.activation(out=tt_[s2][:], in_=po[x][:], func=AF.Copy),
                                 reads=[b_po[x]], writes=[b_tr[s2]])
                            S.mm([(pss[:], P.ones_bf[:], sqr[s2][:], True, True)], reads=[b_sqr[s2], P.b_ones], writes=[b_pss])
                            S.op("act", lambda e: e.activation(out=rr[s2][:], in_=pss[:], func=AF.Sqrt, scale=1.0 / 128, bias=P.eps_col[:]),
                                 reads=[b_pss], writes=[b_rr[s2]])
                            S.op("dve", lambda e: e.reciprocal(out=rr[s2][:], in_=rr[s2][:]), reads=[b_rr[s2]], writes=[b_rr[s2]])
                            S.op("dve", lambda e: e.tensor_tensor(out=tt_[s2][:], in0=tt_[s2][:], in1=rr[s2][:], op=ALU.mult),
                                 reads=[b_rr[s2]], writes=[b_tr[s2]])
                            S.op("dve", lambda e: e.tensor_tensor(out=ystg[s2][:], in0=tt_[s2][:], in1=gbuf[s2][:], op=ALU.mult),
                                 reads=[b_tr[s2], b_g[s2]], writes=[b_ystg[s2]])
                            S.dma("sp", D_["ys"][h * 128:(h + 1) * 128, t0:t0 + TB], ystg[s2][:], reads=[b_ystg[s2]], sem=b_ystg[s2])
            S.barrier()
            if getattr(P, "mix_stop", None) == "R":
                return
    with contextlib.ExitStack() as st:
        cqn = P.sb(st, "cqn", [128, 4, T], BF16)
        ckv = P.sb(st, "ckv", [128, 2, 2 * T], BF16)
        kpe = P.sb(st, "kpe", [64, 2 * T], BF16)
        wuq = P.sb(st, "wuq", [128, 4, 1536], BF16)
        wuq_sw = P.sb(st, "wuq_sw", [128, 4, 8, 64], BF16)
        wukv = P.sb(st, "wukv", [128, 2, 2048], BF16)
        wst = P.sb(st, "wst", [128, 2048], F32)
        gcol = P.sb(st, "gcol", [128, 4], F32)
        knT = [P.sb(st, f"knT{i}", [128, 2 * T], BF16) for i in range(2)]
        vh = [P.sb(st, f"vh{i}", [128, 2 * NT, 128], BF16) for i in range(2)]
        qn = [P.sb(st, f"qn{i}", [128, TB], BF16) for i in range(2)]
        qpe = [P.sb(st, f"qpe{i}", [64, TB], BF16) for i in range(2)]
        pT = [P.sb(st, f"pT{i}", [128, TB], BF16) for i in range(3)]
        t1 = [P.sb(st, f"t1{i}", [64, TB], F32) for i in range(2)]
        rden = [P.sb(st, f"rden{i}", [128, TB], F32) for i in range(2)]
        ystg = [P.sb(st, f"ystga{i}", [128, TB], BF16) for i in range(2)]
        pS = [P.ps(st, f"aS{i}", [128, 512]) for i in range(2)]
        py = [P.ps(st, f"ay{i}", [128, 512]) for i in range(2)]
        pden = [P.ps(st, f"ad{i}", [128, 512]) for i in range(2)]
        pk = [P.ps(st, f"ak{i}", [128, 512]) for i in range(2)]
        b_cqn, b_ckv, b_kpe, b_wuq, b_wukv, b_wst, b_gcol = (S.buf(n) for n in ("cqn", "ckv", "kpe", "wuq", "wukv", "wst", "gcol"))
        b_knT, b_vh, b_rden = S.bufs("knT", 2), S.bufs("vh", 2), S.bufs("rden", 2)
        b_qn, b_qpe, b_t1, b_ystg = S.bufs("qn", 2), S.bufs("qpe", 2), S.bufs("t1", 2), S.bufs("ystga", 2)
        b_pT = S.bufs("pT", 3)
        b_pS, b_py, b_pden, b_pk = S.bufs("aS", 2), S.bufs("ay", 2), S.bufs("ad", 2), S.bufs("ak", 2)
        cmt = P.sb(st, "cmt", [128, 2048], BF16)
        b_cmt = S.buf("cmt")
        S.dma("pool", cmt[:], D_["cmask"], writes=[b_cmt])
        S.dma("sp", cqn[:], D_["cqs"].rearrange("(c p) t -> p c t", p=128), writes=[b_cqn])
        S.dma("sp", ckv[:, :, 0:T], D_["gkv"][0:256, :].rearrange("(c p) t -> p c t", p=128), writes=[b_ckv], sem=b_ckv)
        S.dma("sp", ckv[:, :, T:2 * T], D_["xkv"].rearrange("(c p) t -> p c t", p=128), writes=[b_ckv], sem=b_ckv)
        S.dma("sp", kpe[:, 0:T], D_["gpe"][0:64, :], writes=[b_kpe], sem=b_kpe)
        S.dma("sp", kpe[:, T:2 * T], D_["xpe"], writes=[b_kpe], sem=b_kpe)
        S.op("dve", lambda e: e.tensor_scalar(out=gcol[:], in0=P.smc[:, li, 8:12], scalar1=SM_SCALE, scalar2=None, op0=ALU.mult),
             reads=[P.b_cst], writes=[b_gcol])
        for kc in range(4):
            S.dma("sp", wst[:, 0:1536], w_uq_r[kc], writes=[b_wst])
            S.op("dve", lambda e: e.tensor_scalar(out=wuq[:, kc, :], in0=wst[:, 0:1536], scalar1=gcol[:, kc:kc + 1], scalar2=None, op0=ALU.mult),
                 reads=[b_wst, b_gcol], writes=[b_wuq])
        for kc in range(2):
            S.dma("sp", wst[:, :], w_ukv_r[kc], writes=[b_wst])
            S.op("dve", lambda e: e.tensor_scalar(out=wukv[:, kc, :], in0=wst[:, :], scalar1=P.smc[:, li, 12 + kc:13 + kc], scalar2=None, op0=ALU.mult),
                 reads=[b_wst, P.b_cst], writes=[b_wukv])
        wv = wuq[:, :, :].rearrange("p k (h c) -> p k h c", h=8)
        S.op("dve", lambda e: e.tensor_copy(out=wuq_sw[:, :, :, 0:32], in_=wv[:, :, :, 160:192]), reads=[b_wuq], writes=[b_wuq])
        S.op("dve", lambda e: e.tensor_copy(out=wuq_sw[:, :, :, 32:64], in_=wv[:, :, :, 128:160]), reads=[b_wuq], writes=[b_wuq])
        bg = []

        def kv_tasks(h):
            kb_ = h % 2
            tasks = []
            for kb in range(2 * T // 512):
                def f(kb=kb):
                    s_ = kb % 2
                    S.mm([(pk[s_][:], wukv[:, kc, h * 256: h * 256 + 128], ckv[:, kc, kb * 512:(kb + 1) * 512], kc == 0, kc == 1) for kc in range(2)],
                         reads=[b_wukv, b_ckv], writes=[b_pk[s_]])
                    S.op("dve", lambda e: e.tensor_copy(out=knT[kb_][:, kb * 512:(kb + 1) * 512], in_=pk[s_][:]), reads=[b_pk[s_]], writes=[b_knT[kb_]])
                tasks.append(f)
            for g4 in range(2 * NT // 4):
                def f(g4=g4):
                    s_ = g4 % 2
                    for t4 in range(4):
                        jt = g4 * 4 + t4
                        S.mm([(pk[s_][:, t4 * 128:(t4 + 1) * 128], ckv[:, kc, jt * 128:(jt + 1) * 128], wukv[:, kc, h * 256 + 128: h * 256 + 256], kc == 0, kc == 1)
                              for kc in range(2)], reads=[b_wukv, b_ckv], writes=[b_pk[s_]])
                    dstv = vh[kb_][:, g4 * 4:(g4 + 1) * 4, :].rearrange("p a b -> p (a b)")
                    if g4 * 4 < NT:
                        S.op("dve", lambda e: e.tensor_scalar(out=dstv, in0=pk[s_][:], scalar1=P.flag[:, 0:1], scalar2=None, op0=ALU.mult),
                             reads=[b_pk[s_], P.b_cst], writes=[b_vh[kb_]])
                    else:
                        S.op("dve", lambda e: e.tensor_copy(out=dstv, in_=pk[s_][:]), reads=[b_pk[s_]], writes=[b_vh[kb_]])
                tasks.append(f)
            return tasks

        def q_tasks(h, ib, s):
            qsl = slice(ib * TB, (ib + 1) * TB)

            def f1():
                S.mm([(pk[0][:], wuq[:, kc, h * 192: h * 192 + 128], cqn[:, kc, qsl], kc == 0, kc == 3) for kc in range(4)],
                     reads=[b_wuq, b_cqn], writes=[b_pk[0]])
                S.op("dve", lambda e: e.tensor_copy(out=qn[s][:], in_=pk[0][:]), reads=[b_pk[0]], writes=[b_qn[s]])

            def f2():
                S.mm([(pk[1][0:64, :], wuq[:, kc, h * 192 + 128: h * 192 + 192], cqn[:, kc, qsl], kc == 0, kc == 3) for kc in range(4)],
                     reads=[b_wuq, b_cqn], writes=[b_pk[1]])
                S.op("dve", lambda e: e.tensor_tensor(out=t1[0][:], in0=pk[1][0:64, :], in1=P.cos2[0:64, qsl], op=ALU.mult),
                     reads=[b_pk[1], P.b_rope], writes=[b_t1[0]])

            def f3():
                S.mm([(pk[1][0:64, :], wuq_sw[:, kc, h, :], cqn[:, kc, qsl], kc == 0, kc == 3) for kc in range(4)],
                     reads=[b_wuq, b_cqn], writes=[b_pk[1]])
                S.op("dve", lambda e: e.tensor_tensor(out=t1[1][:], in0=pk[1][0:64, :], in1=P.sin2[0:64, qsl], op=ALU.mult),
                     reads=[b_pk[1], P.b_rope], writes=[b_t1[1]])
                S.op("dve", lambda e: e.tensor_tensor(out=qpe[s][:], in0=t1[0][:], in1=t1[1][:], op=ALU.add),
                     reads=[b_t1[0], b_t1[1]], writes=[b_qpe[s]])
            return [f1, f2, f3]

        seq = [(h, ib) for h in range(8) for ib in range(nblk)]
        for f in kv_tasks(0) + q_tasks(0, 0, 0):
            f()
        nun = 0
        for idx, (h, ib) in enumerate(seq):
            kb_ = h % 2
            s = idx % 2
            q0 = ib * TB
            qsl = slice(q0, q0 + TB)
            if idx + 1 < len(seq):
                bg.extend(q_tasks(seq[idx + 1][0], seq[idx + 1][1], (idx + 1) % 2))
            if ib == 1 and h + 1 < 8:
                bg.extend(kv_tasks(h + 1))
            if True:
                tiles = list(range(NT)) + [NT + j for j in range((ib + 1) * 4)]
                ya = idx % 2
                ubase = nun
                nun += len(tiles)

                def emit_S(ti):
                    jt = tiles[ti]
                    u = (ubase + ti) % 2
                    ksl = slice(jt * 128, (jt + 1) * 128)
                    S.mm([(pS[u][:], knT[kb_][:, ksl], qn[s][:], True, False),
                          (pS[u][:], kpe[0:64, ksl], qpe[s][:], False, True)],
                         reads=[b_knT[kb_], b_kpe, b_qn[s], b_qpe[s]], writes=[b_pS[u]])

                emit_S(0)
                for ti, jt in enumerate(tiles):
                    u = (ubase + ti) % 2
                    pp = (ubase + ti) % 3
                    if ti + 1 < len(tiles):
                        emit_S(ti + 1)
                    if bg and ti >= 1:
                        bg.pop(0)()
                    S.op("act", lambda e: e.activation(out=pT[pp][:], in_=pS[u][:], func=AF.Exp), reads=[b_pS[u]], writes=[b_pT[pp]])
                    r = jt - NT - ib * 4
                    if r >= 0:
                        S.op("dve", lambda e: e.tensor_tensor(out=pT[pp][:], in0=pT[pp][:], in1=cmt[:, r * 512:(r + 1) * 512], op=ALU.mult),
                             reads=[b_cmt], writes=[b_pT[pp]])
                    first, last = ti == 0, ti == len(tiles) - 1
                    onesm = P.flagmat if jt < NT else P.ones_bf
                    S.mm([(py[ya][:], vh[kb_][:, jt, :], pT[pp][:], first, last),
                          (pden[ya][:], onesm[:], pT[pp][:], first, last)],
                         reads=[b_vh[kb_], b_pT[pp], P.b_ones], writes=[b_py[ya], b_pden[ya]])
                while bg and (len(bg) > 16 or ib == nblk - 1 or bg[0].__name__ != "f"):
                    bg.pop(0)()
                S.op("dve", lambda e: e.reciprocal(out=rden[ya][:], in_=pden[ya][:]), reads=[b_pden[ya]], writes=[b_rden[ya]])
                S.op("dve", lambda e: e.tensor_tensor(out=ystg[s][:], in0=py[ya][:], in1=rden[ya][:], op=ALU.mult),
                     reads=[b_py[ya], b_rden[ya]], writes=[b_ystg[s]])
                S.dma("sp", D_["ys"][(8 + h) * 128:(9 + h) * 128, qsl], ystg[s][:], reads=[b_ystg[s]], sem=b_ystg[s])
        S.barrier()
        if getattr(P, "mix_stop", None) == "A":
            return
    with contextlib.ExitStack() as st:
        TBO = 1024
        nbo = T // TBO
        yblk = [P.sb(st, f"yblk{i}", [128, NCH, TBO], BF16) for i in range(2)]
        wo = [P.sb(st, f"wo{i}", [128, NCH, 128], BF16) for i in range(4)]
        xc = [P.sb(st, f"oxc{i}", [128, TBO], F32) for i in range(3)]
        po = [P.ps(st, f"opo{i}", [128, 512]) for i in range(4)]
        b_y, b_wo, b_xc, b_po = S.bufs("yblk", 2), S.bufs("wo", 4), S.bufs("oxc", 3), S.bufs("opo", 4)
        loads = []
        for b in range(nbo):
            for m in range(NCH):
                n = b * NCH + m
                loads.append((b_wo[n % 4], wo[n % 4][:], wo_r[m], "wo", 4))
        strm = Stream(S, "pool", loads)
        g2 = P.gh[:, li, 16:32]
        step = 0

        def ld_x(n):
            if n >= nbo * NCH:
                return
            bb, mm_ = divmod(n, NCH)
            S.dma("sp", xc[n % 3][:], xs[mm_ * 128:(mm_ + 1) * 128, bb * TBO:(bb + 1) * TBO], writes=[b_xc[n % 3]])
        ld_x(0)
        for b in range(nbo):
            t0 = b * TBO
            S.dma("sp", yblk[b % 2][:], D_["ys"][:, t0:t0 + TBO].rearrange("(c p) t -> p c t", p=128), writes=[b_y[b % 2]])
            for m in range(NCH):
                strm.ensure(step + 3)
                n = b * NCH + m
                x3 = n % 3
                ld_x(n + 1)
                for th in range(2):
                    pi = (2 * n + th) % 4
                    tsl = slice(th * 512, (th + 1) * 512)
                    S.mm([(po[pi][:], wo[n % 4][:, kc, :], yblk[b % 2][:, kc, tsl], kc == 0, kc == NCH - 1) for kc in range(NCH)],
                         reads=[b_wo[n % 4], b_y[b % 2]], writes=[b_po[pi]])
                strm.consumed("wo")
                for th in range(2):
                    pi = (2 * n + th) % 4
                    tsl = slice(th * 512, (th + 1) * 512)
                    S.op("dve", lambda e: e.scalar_tensor_tensor(out=xc[x3][:, tsl], in0=po[pi][:], scalar=g2[:, m:m + 1], in1=xc[x3][:, tsl],
                                                                 op0=ALU.mult, op1=ALU.add),
                         reads=[b_po[pi], P.b_mod], writes=[b_xc[x3]])
                S.dma("sp", xs[m * 128:(m + 1) * 128, t0:t0 + TBO], xc[x3][:], reads=[b_xc[x3]], sem=b_xc[x3])
                step += 1
        S.barrier()


def exchange(P, D_, group):
    S = P.S
    g = P.nc.gpsimd
    if "cc" not in S.sems:
        S._newsem("cc")
    for src, dst in (("xkv", "gkv"), ("xpe", "gpe"), ("xst", "gst")):
        ins = g.collective_compute("AllGather", ALU.bypass, replica_groups=group, ins=[D_[src]], outs=[D_[dst]])
        S.cur["cc"] += 1
        ins.then_inc(S.sems["cc"], 1)
    S._wait("pool", [("cc", S.cur["cc"])])
    S.barrier()


def build(cfg):
    nc = bass.Bass("TRN2", target_bir_lowering=False)
    nl = cfg.get("nl", L)
    T = cfg.get("T", TOK)
    ncores = cfg.get("ncores", 8)
    stop = cfg.get("stop", None)
    A = {}

    def inp(name, shape, dt=F32):
        A[name] = nc.dram_tensor(name, shape, dt, kind="ExternalInput").ap()
        return A[name]

    xT = inp("xT", [D, T])
    ccol_all = inp("ccol_all", [128, 16, 4])
    onehot = inp("onehot", [128, 4])
    pos = inp("pos", [1, T], I32)
    flag = inp("flag", [128, 1])
    cst = inp("cst", [128, NCONST])
    ada_n = 4 if ncores == 8 else 2
    CW = NMOD * 128 // ada_n
    w_ada_r = inp("w_ada_r", [nl, 128, 16, CW])
    ada_in = nc.dram_tensor("ada_in", [nl * 128, (NMOD // ada_n) * 4], F32).ap()
    ada_g1 = nc.dram_tensor("ada_g1", [2 * nl * 128, (NMOD // ada_n) * 4], F32).ap()
    ada_g = ada_g1
    if ada_n == 4:
        ada_g = nc.dram_tensor("ada_g2", [4 * nl * 128, (NMOD // ada_n) * 4], F32).ap()
    b_ada_c = inp("b_ada_c", [nl, 128, NMOD])
    ng_c = inp("ng_c", [nl, 3, 128, 16])
    sm_c = inp("sm_c", [nl, 128, 16])
    fin_c = inp("fin_c", [128, 16])
    wgu_r = inp("wgu_r", [nl, 2, NFF, 128, 16, 256])
    wd_r = inp("wd_r", [nl, 2, NCH, 2, 128, NFF // 2, 128])
    w_inF = inp("w_inF", [nl, NSLAB, 128, 2, 16, 128])
    w_inV = inp("w_inV", [nl, 4, 128, 16, 256])
    w_uq_r = inp("w_uq_r", [nl, 4, 128, 1536])
    w_ukv_r = inp("w_ukv_r", [nl, 2, 128, 2048])
    wo_r = inp("wo_r", [nl, NCH, 128, 16, 128])
    outT = nc.dram_tensor("outT", [D, T], F32, kind="ExternalOutput").ap()
    xs = nc.dram_tensor("xs", [D, T], F32).ap()
    dk = "ExternalOutput" if cfg.get("dbg_out") else "Internal"
    cmask = inp("cmask", [128, 2048])
    D_ = {
        "cmask": cmask,
        "cqs": nc.dram_tensor("cqs", [512, T], BF16, kind=dk).ap(),
        "gsd": nc.dram_tensor("gsd", [1024, T], BF16, kind=dk).ap(),
        "ys": nc.dram_tensor("ys", [2048, T], BF16, kind=dk).ap(),
        "xkv": nc.dram_tensor("xkv", [256, T], BF16).ap(),
        "xpe": nc.dram_tensor("xpe", [64, T], BF16).ap(),
        "xst": nc.dram_tensor("xst", [512, 256], BF16).ap(),
        "gkv": nc.dram_tensor("gkv", [512, T], BF16).ap(),
        "gpe": nc.dram_tensor("gpe", [128, T], BF16).ap(),
        "gst": nc.dram_tensor("gst", [1024, 256], BF16).ap(),
    }
    group = [[2 * i, 2 * i + 1] for i in range(ncores // 2)]
    with contextlib.ExitStack() as stack:
        P = Prog(nc, stack, list(range(nl)), T)
        P.mix_stop = cfg.get("mix_stop")
        P.consts()
        mix_consts(P, cst, flag, sm_c)
        stages = [(group, ada_in, ada_g1)]
        if ada_n == 4:
            stages.append(([[i, i + 4] for i in range(4)], ada_g1, ada_g))
        P.adaln(ccol_all, onehot, w_ada_r, b_ada_c, ng_c, ada_in, ada_g, ada_n, stages)
        rope_tables(P, pos)
        fcol = P.sb(stack, "fcol", [128, 16], F32)
        P.S.dma("sp", fcol[:], fin_c, writes=[P.b_cst], sem=P.b_cst)
        done = False
        for li in range(nl):
            src = xT if li == 0 else xs
            if stop == ("ffn1", li):
                P.ffn(li, 0, src, outT, wgu_r[li, 0], wd_r[li, 0])
                break
            P.ffn(li, 0, src, xs, wgu_r[li, 0], wd_r[li, 0])
            mixer(P, li, xs, w_inF[li], w_inV[li], w_uq_r[li], w_ukv_r[li], wo_r[li], D_, group)
            if stop == ("mixer", li):
                P.ffn_copy(xs, outT)
                break
            last = (li == nl - 1) and stop is None
            if stop == ("layer", li):
                P.ffn(li, 1, xs, outT, wgu_r[li, 1], wd_r[li, 1])
                break
            P.ffn(li, 1, xs, xs, wgu_r[li, 1], wd_r[li, 1], final=(fcol, outT) if last else None)
        P.S.finish([])
    return nc


def _copy_phase(P, src, dst):
    S = P.S
    with contextlib.ExitStack() as st:
        t = [P.sb(st, f"cp{i}", [128, TB * 4], F32) for i in range(2)]
        bt = S.bufs("cp", 2)
        n = 0
        for c in range(NCH):
            for t0 in range(0, P.T, TB * 4):
                w = min(TB * 4, P.T - t0)
                S.dma("sp", t[n % 2][:, :w], src[c * 128:(c + 1) * 128, t0:t0 + w], writes=[bt[n % 2]])
                S.dma("sp", dst[c * 128:(c + 1) * 128, t0:t0 + w], t[n % 2][:, :w], reads=[bt[n % 2]], sem=bt[n % 2])
                n += 1
        S.barrier()


Prog.ffn_copy = _copy_phase


def _cols_swapped(base):
    idx = []
    for h in range(2):
        b = base + h * 64
        idx += list(range(b + 32, b + 64)) + list(range(b, b + 32))
    return idx


def prep_shared(inp, layers):
    f = lambda a: np.ascontiguousarray(np.asarray(a, dtype=np.float32))
    out = {}
    ls = list(layers)
    out["cst"] = host_consts()
    out["cmask"] = host_cmask()
    out["b_ada_c"] = f(np.stack([np.asarray(inp["b_ada"][l]).reshape(NMOD, 128).T for l in ls]))
    out["ng_c"] = f(np.stack([np.stack([np.asarray(inp[n][l]).reshape(16, 128).T for n in ("norm_ffn1", "norm_mix", "norm_ffn2")])
                              for l in ls]))
    out["fin_c"] = f(np.asarray(inp["final_norm"]).reshape(16, 128).T)
    sm = np.zeros((len(ls), 128, 16), np.float32)
    for i, l in enumerate(ls):
        sm[i, :, 0:8] = np.asarray(inp["ret_norm_g"][l]).reshape(8, 128).T
        sm[i, :, 8:12] = np.asarray(inp["q_norm_g"][l]).reshape(4, 128).T
        sm[i, :, 12:14] = np.asarray(inp["kv_norm_g"][l]).reshape(2, 128).T
    out["sm_c"] = sm
    wg, wdn, wF, wV, wuq, wukv, wo = [], [], [], [], [], [], []
    for l in ls:
        a, b_ = [], []
        for nm_gu, nm_d in (("ffn1_w_gu", "ffn1_w_down"), ("ffn2_w_gu", "ffn2_w_down")):
            w = np.asarray(inp[nm_gu][l])
            g = w[:, :DFF].reshape(16, 128, NFF, 128)
            u = w[:, DFF:].reshape(16, 128, NFF, 128)
            gu = np.concatenate([g, u], axis=3)
            a.append(gu.transpose(2, 1, 0, 3))
            wdd = np.asarray(inp[nm_d][l]).reshape(2, NFF // 2, 128, 16, 128)
            b_.append(wdd.transpose(3, 0, 2, 1, 4))
        wg.append(np.stack(a))
        wdn.append(np.stack(b_))
        wi = np.asarray(inp["w_in"][l])
        colsets = []
        for sec in (0, 512):
            for hp in range(4):
                base = sec + hp * 128
                colsets.append((list(range(base, base + 128)), _cols_swapped(base)))
        for i in range(4):
            colsets.append((list(range(2048 + 2 * i * 128, 2048 + (2 * i + 1) * 128)),
                            list(range(2048 + (2 * i + 1) * 128, 2048 + (2 * i + 2) * 128))))
        for i in range(2):
            colsets.append((list(range(3072 + 2 * i * 128, 3072 + (2 * i + 1) * 128)),
                            list(range(3072 + (2 * i + 1) * 128, 3072 + (2 * i + 2) * 128))))
        colsets.append((list(range(3584, 3712)), list(range(3712, 3840))))
        kp = list(range(3840, 3904))
        kps = list(range(3872, 3904)) + list(range(3840, 3872))
        colsets.append((kp + kp, kps + kps))
        slabs = []
        for c0, c1 in colsets:
            two = []
            for cs in (c0, c1):
                two.append(wi[:, cs].reshape(16, 128, 128).transpose(1, 0, 2))
            slabs.append(np.stack(two, axis=1))
        wF.append(np.stack(slabs))
        wV.append(np.stack([wi[:, 1024 + vs * 256: 1024 + (vs + 1) * 256].reshape(16, 128, 256).transpose(1, 0, 2) for vs in range(4)]))
        wuq.append(np.asarray(inp["w_uq"][l]).reshape(4, 128, 1536))
        wukv.append(np.asarray(inp["w_ukv"][l]).reshape(2, 128, 2048))
        wo_l = np.asarray(inp["w_out"][l]).reshape(16, 128, 16, 128)
        wo.append(wo_l.transpose(2, 1, 0, 3))
    out["wgu_r"] = f(np.stack(wg))
    out["wd_r"] = f(np.stack(wdn))
    out["w_inF"] = f(np.stack(wF))
    out["w_inV"] = f(np.stack(wV))
    out["w_uq_r"] = f(np.stack(wuq))
    out["w_ukv_r"] = f(np.stack(wukv))
    out["wo_r"] = f(np.stack(wo))
    return out


def prep_core(inp, core, T=TOK, ncores=8, layers=(0, 1)):
    b, half = core // 2, core % 2
    x = np.asarray(inp["x"])[b, half * TOK: half * TOK + T, :]
    ada_n = 4 if ncores == 8 else 2
    CW = NMOD * 128 // ada_n
    sl_i = (core // 4) * 2 + core % 2 if ncores == 8 else core % 2
    oh = np.zeros((128, 4), np.float32)
    oh[:, b] = 1.0
    d = {"xT": np.ascontiguousarray(x.T.astype(np.float32)),
         "ccol_all": np.ascontiguousarray(np.asarray(inp["c"], dtype=np.float32).reshape(4, 16, 128).transpose(2, 1, 0)),
         "onehot": oh,
         "w_ada_r": np.ascontiguousarray(np.stack([np.asarray(inp["w_ada"][l])[:, sl_i * CW:(sl_i + 1) * CW]
                                                   .reshape(16, 128, CW).transpose(1, 0, 2) for l in layers]).astype(np.float32)),
         "pos": np.ascontiguousarray(np.asarray(inp["positions"])[b, half * TOK: half * TOK + T].reshape(1, T).astype(np.int32)),
         "flag": np.full((128, 1), float(half), np.float32)}
    return d


_NC_CACHE = {}


def kernel(**inputs):
    ncores = 8
    if "nc" not in _NC_CACHE:
        _NC_CACHE["nc"] = build({"nl": L, "T": TOK, "ncores": ncores})
    nc = _NC_CACHE["nc"]
    sh = prep_shared(inputs, range(L))
    maps = []
    for c in range(ncores):
        m = dict(sh)
        m.update(prep_core(inputs, c, TOK, ncores, range(L)))
        maps.append(m)
    res = run_bass_kernel_spmd(nc, maps, core_ids=list(range(ncores)))
    x = np.asarray(inputs["x"])
    out = np.empty(x.shape, np.float32)
    for c in range(ncores):
        b, half = c // 2, c % 2
        out[b, half * TOK:(half + 1) * TOK, :] = np.asarray(res.results[c]["outT"]).T
    return out
```

```python
import contextlib
import numpy as np
import concourse.bass as bass
import concourse.mybir as mybir
from concourse.bass_utils import run_bass_kernel_spmd

F32 = mybir.dt.float32
BF16 = mybir.dt.bfloat16
I32 = mybir.dt.int32
AF = mybir.ActivationFunctionType
ALU = mybir.AluOpType

D = 2048
NCH = 16
DFF = 5632
NFF = 44
TOK = 2048
TB = 512
NBLK = TOK // TB
EPS = 1e-6
L = 2
NMOD = 144
IN_COLS = 3904


PSUM_NAMES = {"pm", "pst", "pg", "pu", "po", "pa", "pb", "pv", "pt", "pS", "pinc", "pss", "aS", "ay", "ad", "ak", "opo"}


class Buf:
    __slots__ = ("name", "w", "rs", "dsem", "dcount", "psum")

    def __init__(self, name):
        self.name = name
        self.psum = name.rstrip("0123456789") in PSUM_NAMES
        self.w = None
        self.rs = {}
        self.dsem = None
        self.dcount = 0


class Sched:
    ENG = ("pe", "act", "dve", "pool", "sp")

    def __init__(self, nc, stack):
        self.nc = nc
        self.stack = stack
        self.e = {"pe": nc.tensor, "act": nc.scalar, "dve": nc.vector, "pool": nc.gpsimd, "sp": nc.sync}
        self.sems = {}
        self.cur = {}
        self.known = {k: {} for k in self.ENG}
        for k in self.ENG:
            self._newsem(k)
        self.ndsem = 0
        self.dma_bufs = []
        self.free_dsems = []

    def _newsem(self, key):
        self.sems[key] = self.stack.enter_context(self.nc.semaphore("s_" + key))
        self.cur[key] = 0

    def buf(self, name):
        return Buf(name)

    def bufs(self, name, n):
        return [Buf(f"{name}{i}") for i in range(n)]

    def _wait(self, eng, deps):
        need = {}
        for d in deps:
            if d is None:
                continue
            k, v = d
            if k == eng and eng == "pe":
                continue
            if v > need.get(k, 0):
                need[k] = v
        kn = self.known[eng]
        for k, v in need.items():
            if kn.get(k, 0) >= v:
                continue
            self.e[eng].wait_ge(self.sems[k], v)
            kn[k] = v

    def _deps(self, reads, writes):
        deps = []
        for b in reads:
            deps.append(b.w)
            if b.psum:
                deps.extend(b.rs.items())
        for b in writes:
            deps.append(b.w)
            deps.extend(b.rs.items())
        return deps

    def _commit(self, ev, reads, writes):
        k, v = ev
        for b in reads:
            if b.rs.get(k, 0) < v:
                b.rs[k] = v
        for b in writes:
            b.w = ev
            b.rs = {}

    def op(self, eng, fn, reads=(), writes=()):
        self._wait(eng, self._deps(reads, writes))
        ins = fn(self.e[eng])
        self.cur[eng] += 1
        ins.then_inc(self.sems[eng], 1)
        self._commit((eng, self.cur[eng]), reads, writes)

    def mm(self, mms, reads=(), writes=()):
        self._wait("pe", self._deps(reads, writes))
        pe = self.e["pe"]
        ins = None
        for (o, a, b, st, sp) in mms:
            ins = pe.matmul(o, lhsT=a, rhs=b, start=st, stop=sp)
        self.cur["pe"] += 1
        ins.then_inc(self.sems["pe"], 1)
        self._commit(("pe", self.cur["pe"]), reads, writes)

    def pe_ops(self, fns, reads=(), writes=()):
        self._wait("pe", self._deps(reads, writes))
        pe = self.e["pe"]
        ins = None
        for f in fns:
            ins = f(pe)
        self.cur["pe"] += 1
        ins.then_inc(self.sems["pe"], 1)
        self._commit(("pe", self.cur["pe"]), reads, writes)

    def dma(self, q, out, in_, reads=(), writes=(), sem=None):
        if sem is None:
            sem = writes[0] if writes else reads[0]
        if sem.dsem is None:
            if self.free_dsems:
                key = self.free_dsems.pop()
            else:
                key = f"d{self.ndsem}"
                self.ndsem += 1
                self._newsem(key)
            sem.dsem = key
            self.dma_bufs.append(sem)
        key = sem.dsem
        deps = self._deps(reads, writes)
        if self.cur[key] > 0:
            deps.append((key, self.cur[key]))
        self._wait(q, deps)
        ins = self.e[q].dma_start(out=out, in_=in_)
        self.cur[key] += 16
        ins.then_inc(self.sems[key], 16)
        self._commit((key, self.cur[key]), reads, writes)

    def barrier(self):
        if "bar" not in self.sems:
            self._newsem("bar")
        deps = [(k, v) for k, v in self.cur.items() if k not in ("bar", "sp") and v > 0]
        self._wait("sp", deps)
        self.cur["bar"] += 1
        self.e["sp"].sem_inc(self.sems["bar"], 1)
        for eng in self.ENG:
            if eng == "sp":
                continue
            self._wait(eng, [("bar", self.cur["bar"])])
        for eng in self.ENG:
            for k, v in self.cur.items():
                self.known[eng][k] = v
        for b in self.dma_bufs:
            self.free_dsems.append(b.dsem)
            b.dsem = None
        self.dma_bufs = []

    def finish(self, outs):
        deps = [(k, v) for k, v in self.cur.items() if k not in ("bar", "sp") and v > 0]
        self._wait("sp", deps)


class Stream:
    def __init__(self, S, q, loads):
        self.S, self.q, self.loads, self.nxt = S, q, loads, 0
        self.done = {}
        self.ridx = []
        cnt = {}
        for ld in loads:
            r = ld[3]
            self.ridx.append(cnt.get(r, 0))
            cnt[r] = cnt.get(r, 0) + 1

    def ensure(self, i):
        i = min(i, len(self.loads) - 1)
        while self.nxt <= i:
            buf, out_ap, in_ap, ring, nslots = self.loads[self.nxt]
            if self.ridx[self.nxt] >= self.done.get(ring, 0) + nslots:
                break
            self.S.dma(self.q, out_ap, in_ap, writes=[buf])
            self.nxt += 1

    def consumed(self, ring):
        self.done[ring] = self.done.get(ring, 0) + 1


class Prog:
    def __init__(self, nc, stack, layers, T=TOK):
        self.nc, self.stack = nc, stack
        self.S = Sched(nc, stack)
        self.layers = layers
        self.T = T
        self.nblk = T // TB

    _uid = 0

    def sb(self, st, name, shape, dt):
        Prog._uid += 1
        return st.enter_context(self.nc.sbuf_tensor(f"{name}_s{Prog._uid}", shape, dt))

    def ps(self, st, name, shape, dt=F32):
        Prog._uid += 1
        return st.enter_context(self.nc.psum_tensor(f"{name}_p{Prog._uid}", shape, dt))

    def consts(self):
        S, st = self.S, self.stack
        self.ones_bf = self.sb(st, "ones_bf", [128, 128], BF16)
        self.b_ones = S.buf("ones")
        S.op("dve", lambda e: e.memset(self.ones_bf[:], 1.0), writes=[self.b_ones])
        nl = len(self.layers)
        self.mc = self.sb(st, "mc", [128, nl, NMOD], F32)
        self.gs = self.sb(st, "gs", [128, nl, 48], F32)
        self.gh = self.sb(st, "gh", [128, nl, 48], F32)
        self.b_mod = S.buf("mod")
        self.eps_col = self.sb(st, "eps_col", [128, 1], F32)
        S.op("dve", lambda e: e.memset(self.eps_col[:], EPS), writes=[self.b_ones])

    def adaln(self, ccol_all, onehot, w_ada_r, b_ada_c, ng_c, ada_in, ada_g, ncores, stages):
        S, nc = self.S, self.nc
        nl = len(self.layers)
        NJc = NMOD // ncores
        JG = 3
        ngrp = NJc // JG
        NWA = 4
        with contextlib.ExitStack() as st:
            cc = self.sb(st, "cc", [128, 16, 4], F32)
            cact = self.sb(st, "cact", [128, 16, 4], BF16)
            sig = self.sb(st, "sig", [128, 16, 4], F32)
            oh = self.sb(st, "oh", [128, 4], F32)
            bada = self.sb(st, "bada", [128, nl, NMOD], F32)
            ngs = self.sb(st, "ngs", [128, nl, 48], F32)
            modp = self.sb(st, "modp", [128, nl, NJc * 4], F32)
            modg = self.sb(st, "modg", [128, nl, ncores, NJc, 4], F32)
            wa = [self.sb(st, f"wa{i}", [128, 16, JG * 128], BF16) for i in range(NWA)]
            pm = [self.ps(st, f"pm{i}", [128, 512], F32) for i in range(nl)]
            b_cc, b_cact, b_sig, b_bada, b_ngs, b_modp, b_modg = (S.buf(n) for n in ("cc", "cact", "sig", "bada", "ngs", "modp", "modg"))
            b_pm = S.bufs("pm", nl)
            b_wa = S.bufs("wa", NWA)
            S.dma("sp", cc[:], ccol_all, writes=[b_cc])
            S.dma("sp", oh[:], onehot, writes=[b_cc], sem=b_cc)
            for li, l in enumerate(self.layers):
                S.dma("sp", bada[:, li, :], b_ada_c[l], writes=[b_bada], sem=b_bada)
                S.dma("sp", ngs[:, li, :].rearrange("p (k c) -> p k c", k=3),
                      ng_c[l].rearrange("k p c -> p k c"), writes=[b_ngs], sem=b_ngs)
            S.op("act", lambda e: e.activation(out=sig[:], in_=cc[:], func=AF.Sigmoid), reads=[b_cc], writes=[b_sig])
            S.op("dve", lambda e: e.tensor_tensor(out=cact[:], in0=cc[:], in1=sig[:], op=ALU.mult),
                 reads=[b_cc, b_sig], writes=[b_cact])
            loads = []
            for li, l in enumerate(self.layers):
                for g in range(ngrp):
                    i = li * ngrp + g
                    loads.append((b_wa[i % NWA], wa[i % NWA][:], w_ada_r[l][:, :, g * JG * 128:(g + 1) * JG * 128], "wa", NWA))
            strm = Stream(S, "pool", loads)
            for li, l in enumerate(self.layers):
                for g in range(ngrp):
                    i = li * ngrp + g
                    strm.ensure(i + 3)
                    w = wa[i % NWA]
                    mms = []
                    for jj in range(JG):
                        j = g * JG + jj
                        for kc in range(16):
                            mms.append((pm[li][:, j * 4:(j + 1) * 4], w[:, kc, jj * 128:(jj + 1) * 128], cact[:, kc, :], kc == 0, kc == 15))
                    S.mm(mms, reads=[b_wa[i % NWA], b_cact], writes=[b_pm[li]])
                    strm.consumed("wa")
                S.op("dve", lambda e: e.tensor_copy(out=modp[:, li, :], in_=pm[li][:, 0:NJc * 4]), reads=[b_pm[li]], writes=[b_modp])
            S.dma("sp", ada_in.rearrange("(l p) c -> p l c", p=128), modp[:], reads=[b_modp], sem=b_modp)
            S.barrier()
            if "cc" not in S.sems:
                S._newsem("cc")
            for (grp, src_, dst_) in stages:
                ins = nc.gpsimd.collective_compute("AllGather", ALU.bypass, replica_groups=grp, ins=[src_], outs=[dst_])
                S.cur["cc"] += 1
                ins.then_inc(S.sems["cc"], 1)
                S._wait("pool", [("cc", S.cur["cc"])])
            S.barrier()
            agv = ada_g.rearrange("(r l p) c -> p l r c", p=128, l=nl)
            for li in range(nl):
                S.dma("sp", modg[:, li].rearrange("p r j b -> p r (j b)"), agv[:, li], writes=[b_modg], sem=b_modg)
            for li, l in enumerate(self.layers):
                mview = self.mc[:, li, :].rearrange("p (r j) -> p r j", r=ncores)
                for bb in range(4):
                    src = modg[:, li, :, :, bb]
                    if bb == 0:
                        S.op("dve", lambda e: e.tensor_scalar(out=mview, in0=src, scalar1=oh[:, 0:1], scalar2=None, op0=ALU.mult),
                             reads=[b_modg, b_cc], writes=[self.b_mod])
                    else:
                        S.op("dve", lambda e: e.scalar_tensor_tensor(out=mview, in0=src, scalar=oh[:, bb:bb + 1], in1=mview,
                                                                     op0=ALU.mult, op1=ALU.add),
                             reads=[b_modg, b_cc], writes=[self.b_mod])
                S.op("dve", lambda e: e.tensor_tensor(out=self.mc[:, li, :], in0=self.mc[:, li, :], in1=bada[:, li, :], op=ALU.add),
                     reads=[b_bada], writes=[self.b_mod])
                for k in range(3):
                    sc = self.mc[:, li, (3 * k + 1) * 16:(3 * k + 2) * 16]
                    gt = self.mc[:, li, (3 * k + 2) * 16:(3 * k + 3) * 16]
                    S.op("dve", lambda e: e.scalar_tensor_tensor(out=self.gs[:, li, k * 16:(k + 1) * 16], in0=sc, scalar=1.0,
                                                                 in1=ngs[:, li, k * 16:(k + 1) * 16], op0=ALU.add, op1=ALU.mult),
                         reads=[b_ngs, self.b_mod], writes=[self.b_mod])
                    S.op("dve", lambda e: e.tensor_scalar(out=self.gh[:, li, k * 16:(k + 1) * 16], in0=gt,
                                                          scalar1=(1.0 if k == 1 else 0.5), scalar2=None, op0=ALU.mult),
                         reads=[self.b_mod], writes=[self.b_mod])
            S.barrier()

    def rms_stats(self, xblk, b_x, sq, b_sq, pst, b_pst, rstd, b_rstd, nfeat_chunks, tb, nfeat):
        S = self.S
        for c in range(nfeat_chunks):
            s = c % len(sq)
            S.op("act", lambda e: e.activation(out=sq[s][:, :tb], in_=xblk[:, c, :], func=AF.Square),
                 reads=[b_x], writes=[b_sq[s]])
            S.mm([(pst[:, :tb], self.ones_bf[:], sq[s][:, :tb], c == 0, c == nfeat_chunks - 1)],
                 reads=[b_sq[s], self.b_ones], writes=[b_pst])
        S.op("act", lambda e: e.activation(out=rstd[:, :tb], in_=pst[:, :tb], func=AF.Sqrt, scale=1.0 / nfeat, bias=self.eps_col[:]),
             reads=[b_pst], writes=[b_rstd])
        S.op("dve", lambda e: e.reciprocal(out=rstd[:, :tb], in_=rstd[:, :tb]), reads=[b_rstd], writes=[b_rstd])

    def ffn(self, li, which, xsrc, xdst, wgu_r, wd_r, final=None, dbg=None):
        S, nc = self.S, self.nc
        k = 0 if which == 0 else 2
        TBF = 1024
        nblk = self.T // TBF
        NJ = NFF // 2
        with contextlib.ExitStack() as st:
            hT = self.sb(st, "hT", [128, NCH, TBF], BF16)
            actT = self.sb(st, "actT", [128, NJ, TBF], BF16)
            NG, ND = 3, 3
            wgu = [self.sb(st, f"wgu{i}", [128, NCH, 256], BF16) for i in range(NG)]
            wd = [self.sb(st, f"wd{i}", [128, NJ, 128], BF16) for i in range(ND)]
            xr = [self.sb(st, f"xr{i}", [128, TBF], F32) for i in range(3)]
            xp = [self.sb(st, f"xp{i}", [128, TBF], F32) for i in range(2)]
            sq = [self.sb(st, f"sq{i}", [128, TBF], BF16) for i in range(2)]
            tmp = [self.sb(st, f"tmp{i}", [128, TBF], F32) for i in range(2)]
            sg = [self.sb(st, f"sg{i}", [128, TB], F32) for i in range(2)]
            rstd = self.sb(st, "rstd", [128, TBF], F32)
            pst = [self.ps(st, f"pst{i}", [128, 512]) for i in range(2)]
            pg = [self.ps(st, f"pg{i}", [128, 512]) for i in range(2)]
            pu = [self.ps(st, f"pu{i}", [128, 512]) for i in range(2)]
            po = [self.ps(st, f"po{i}", [128, 512]) for i in range(2)]
            b_h, b_rstd, b_pst = S.buf("h"), S.buf("rstd"), S.bufs("pst", 2)
            b_act = S.bufs("act", NJ)
            b_wgu, b_wd = S.bufs("wgu", NG), S.bufs("wd", ND)
            b_xr, b_xp = S.bufs("xr", 3), S.bufs("xp", 2)
            b_sq, b_tmp, b_sg = S.bufs("sq", 2), S.bufs("tmp", 2), S.bufs("sg", 2)
            b_pg, b_pu, b_po = S.bufs("pg", 2), S.bufs("pu", 2), S.bufs("po", 2)
            b_dr = [[S.buf(f"dr{b}_{m}") for m in range(NCH)] for b in range(nblk)]
            loads = []
            ngu = nwd = 0
            for b in range(nblk):
                for half in range(2):
                    for jj in range(NJ):
                        loads.append((b_wgu[ngu % NG], wgu[ngu % NG][:], wgu_r[half * NJ + jj], "gu", NG))
                        ngu += 1
                    for m in range(NCH):
                        loads.append((b_wd[nwd % ND], wd[nwd % ND][:], wd_r[m, half], "wd", ND))
                        nwd += 1
            strm = Stream(S, "pool", loads)
            gsl = self.gs[:, li, k * 16:(k + 1) * 16]
            shl = self.mc[:, li, (3 * k) * 16:(3 * k + 1) * 16]
            ghl = self.gh[:, li, k * 16:(k + 1) * 16]
            npc = [0]

            def pro_load(b, c):
                s = npc[0] % 2
                npc[0] += 1
                S.dma("sp", xp[s][:], xsrc[c * 128:(c + 1) * 128, b * TBF:(b + 1) * TBF], writes=[b_xp[s]])
                return s

            def pro_a(b, c, s):
                S.op("act", lambda e: e.activation(out=sq[s][:], in_=xp[s][:], func=AF.Square), reads=[b_xp[s]], writes=[b_sq[s]])
                for th in range(2):
                    S.mm([(pst[th][:], self.ones_bf[:], sq[s][:, th * 512:(th + 1) * 512], c == 0, c == NCH - 1)],
                         reads=[b_sq[s], self.b_ones], writes=[b_pst[th]])

            def pro_r(b):
                for th in range(2):
                    S.op("act", lambda e: e.activation(out=rstd[:, th * 512:(th + 1) * 512], in_=pst[th][:], func=AF.Sqrt, scale=1.0 / D,
                                                       bias=self.eps_col[:]), reads=[b_pst[th]], writes=[b_rstd])
                S.op("dve", lambda e: e.reciprocal(out=rstd[:], in_=rstd[:]), reads=[b_rstd], writes=[b_rstd])

            def pro_b(b, c, s):
                S.op("dve", lambda e: e.scalar_tensor_tensor(out=tmp[s][:], in0=xp[s][:], scalar=gsl[:, c:c + 1], in1=rstd[:],
                                                             op0=ALU.mult, op1=ALU.mult),
                     reads=[b_xp[s], b_rstd, self.b_mod], writes=[b_tmp[s]])
                S.op("act", lambda e: e.activation(out=hT[:, c, :], in_=tmp[s][:], func=AF.Identity, bias=shl[:, c:c + 1], scale=1.0),
                     reads=[b_tmp[s], self.b_mod], writes=[b_h])

            nres = [0]
            res_slot = {}

            def ld_res(b, half, m):
                x3 = nres[0] % 3
                nres[0] += 1
                res_slot[(b, half, m)] = x3
                src = xsrc if half == 0 else xdst
                S.dma("sp", xr[x3][:], src[m * 128:(m + 1) * 128, b * TBF:(b + 1) * TBF],
                      reads=[b_dr[b][m]] if half == 1 else [], writes=[b_xr[x3]])

            strm.ensure(2)
            for c in range(NCH):
                pro_a(0, c, pro_load(0, c))
            pro_r(0)
            for c in range(NCH):
                pro_b(0, c, pro_load(0, c))
            step = 0
            npsum = 0
            for b in range(nblk):
                for half in range(2):
                    for jj in range(NJ):
                        strm.ensure(step + 2)
                        w = wgu[(step_gu := (b * 2 + half) * NJ + jj) % NG]
                        bw = b_wgu[step_gu % NG]
                        early = (half == 1 and b + 1 < nblk)
                        pcs_a = []
                        if early and jj < 8:
                            pcs_a = [(c, pro_load(b + 1, c)) for c in (2 * jj, 2 * jj + 1)]
                        for th in range(2):
                            s = npsum % 2
                            npsum += 1
                            tsl = slice(th * 512, (th + 1) * 512)
                            S.mm([(pg[s][:], w[:, c, 0:128], hT[:, c, tsl], c == 0, c == NCH - 1) for c in range(NCH)],
                                 reads=[bw, b_h], writes=[b_pg[s]])
                            S.mm([(pu[s][:], w[:, c, 128:256], hT[:, c, tsl], c == 0, c == NCH - 1) for c in range(NCH)],
                                 reads=[bw, b_h], writes=[b_pu[s]])
                            S.op("act", lambda e: e.activation(out=sg[s][:], in_=pg[s][:], func=AF.Silu),
                                 reads=[b_pg[s]], writes=[b_sg[s]])
                            S.op("dve", lambda e: e.tensor_tensor(out=actT[:, jj, tsl], in0=sg[s][:], in1=pu[s][:], op=ALU.mult),
                                 reads=[b_sg[s], b_pu[s]], writes=[b_act[jj]])
                        strm.consumed("gu")
                        step += 1
                        if early:
                            for c, ps_ in pcs_a:
                                pro_a(b + 1, c, ps_)
                            if jj == 8:
                                pro_r(b + 1)
                    ld_res(b, half, 0)
                    for m in range(NCH):
                        strm.ensure(step + 2)
                        n_wd = (b * 2 + half) * NCH + m
                        w = wd[n_wd % ND]
                        if m + 1 < NCH:
                            ld_res(b, half, m + 1)
                        overlap = (half == 1 and b + 1 < nblk)
                        pcs = []
                        if overlap and m < 8:
                            for c in (2 * m, 2 * m + 1):
                                pcs.append((c, pro_load(b + 1, c)))
                        for th in range(2):
                            tsl = slice(th * 512, (th + 1) * 512)
                            S.mm([(po[th][:], w[:, jj, :], actT[:, jj, tsl], jj == 0, jj == NJ - 1) for jj in range(NJ)],
                                 reads=[b_wd[n_wd % ND]] + b_act, writes=[b_po[th]])
                        strm.consumed("wd")
                        if overlap:
                            for c, ps_ in pcs:
                                pro_b(b + 1, c, ps_)
                        x3 = res_slot[(b, half, m)]
                        for th in range(2):
                            tsl = slice(th * 512, (th + 1) * 512)
                            S.op("dve", lambda e: e.scalar_tensor_tensor(out=xr[x3][:, tsl], in0=po[th][:], scalar=ghl[:, m:m + 1],
                                                                         in1=xr[x3][:, tsl], op0=ALU.mult, op1=ALU.add),
                                 reads=[b_po[th], self.b_mod], writes=[b_xr[x3]])
                        S.dma("sp", xdst[m * 128:(m + 1) * 128, b * TBF:(b + 1) * TBF], xr[x3][:], reads=[b_xr[x3]],
                              writes=[b_dr[b][m]], sem=b_xr[x3])
                        step += 1
            S.barrier()
        if final is not None:
            self.final_norm(xdst, final[0], final[1])

    def final_norm(self, xsrc, fcol, outd):
        S = self.S
        with contextlib.ExitStack() as st:
            xk = [self.sb(st, f"xk{i}", [128, NCH, TB], F32) for i in range(2)]
            sq = [self.sb(st, f"fsq{i}", [128, TB], BF16) for i in range(2)]
            oc = [self.sb(st, f"foc{i}", [128, TB], F32) for i in range(4)]
            rstd = [self.sb(st, f"frstd{i}", [128, TB], F32) for i in range(2)]
            pst = [self.ps(st, f"pst{i}", [128, 512]) for i in range(2)]
            b_xk = [S.bufs(f"xk{i}_", NCH) for i in range(2)]
            b_sq, b_oc = S.bufs("fsq", 2), S.bufs("foc", 4)
            b_rstd, b_pst = S.bufs("frstd", 2), S.bufs("pst", 2)
            n = 0

            def ld_blk(b):
                if b >= self.nblk:
                    return
                for c in range(NCH):
                    S.dma("sp", xk[b % 2][:, c, :], xsrc[c * 128:(c + 1) * 128, b * TB:(b + 1) * TB], writes=[b_xk[b % 2][c]])
            ld_blk(0)
            for b in range(self.nblk):
                tsl = slice(b * TB, (b + 1) * TB)
                k2 = b % 2
                ld_blk(b + 1)
                for c in range(NCH):
                    s = c % 2
                    S.op("act", lambda e: e.activation(out=sq[s][:], in_=xk[k2][:, c, :], func=AF.Square), reads=[b_xk[k2][c]], writes=[b_sq[s]])
                    S.mm([(pst[k2][:], self.ones_bf[:], sq[s][:], c == 0, c == NCH - 1)], reads=[b_sq[s], self.b_ones], writes=[b_pst[k2]])
                S.op("act", lambda e: e.activation(out=rstd[k2][:], in_=pst[k2][:], func=AF.Sqrt, scale=1.0 / D, bias=self.eps_col[:]),
                     reads=[b_pst[k2]], writes=[b_rstd[k2]])
                S.op("dve", lambda e: e.reciprocal(out=rstd[k2][:], in_=rstd[k2][:]), reads=[b_rstd[k2]], writes=[b_rstd[k2]])
                for c in range(NCH):
                    o3 = n % 4
                    n += 1
                    S.op("dve", lambda e: e.scalar_tensor_tensor(out=oc[o3][:], in0=xk[k2][:, c, :], scalar=fcol[:, c:c + 1], in1=rstd[k2][:],
                                                                 op0=ALU.mult, op1=ALU.mult),
                         reads=[b_xk[k2][c], b_rstd[k2], self.b_cst], writes=[b_oc[o3]])
                    S.dma("pool", outd[c * 128:(c + 1) * 128, tsl], oc[o3][:], reads=[b_oc[o3]], sem=b_oc[o3])
            S.barrier()

C_INV, C_SIGN, C_NEGPI, C_CD = 0, 1, 2, 3
C_MASK = 8
C_KDEC = C_MASK + 1024
C_QDEC = C_KDEC + 512
C_ID = C_QDEC + 512
NCONST = C_ID + 128
PI = float(np.pi)
NSLAB = 16
RET_H = 8
SM_SCALE = float(192 ** -0.5)


def host_consts():
    c = np.zeros((128, NCONST), np.float32)
    p = np.arange(128)
    inv = (10000.0 ** (-np.arange(0, 64, 2, dtype=np.float32) / 64)).astype(np.float32)
    c[:, C_INV] = inv[p % 32]
    c[:, C_SIGN] = np.where((p % 64) < 32, -1.0, 1.0)
    c[:, C_NEGPI] = -np.pi
    log_g = np.log1p(-(2.0 ** (-5.0 - np.arange(8, dtype=np.float32)))).astype(np.float32)
    idx = np.arange(128, dtype=np.float32)
    for hp in range(4):
        for half in range(2):
            h = 2 * hp + half
            rows = slice(half * 64, half * 64 + 64)
            c[rows, C_CD + hp] = np.exp(log_g[h] * 128.0)
            c[rows, C_QDEC + hp * 128: C_QDEC + (hp + 1) * 128] = np.exp(log_g[h] * (idx + 1.0))[None, :]
            rel = idx[None, :] - idx[:, None]
            m = np.where(rel >= 0, np.exp(log_g[h] * np.maximum(rel, 0.0)), 0.0) * 0.125
            c[:, C_MASK + hp * 256 + half * 128: C_MASK + hp * 256 + (half + 1) * 128] = m
    for h in range(8):
        c[:, C_KDEC + h * 64: C_KDEC + (h + 1) * 64] = (0.125 * np.exp(log_g[h] * (127.0 - idx)))[:, None]
    c[:, C_ID: C_ID + 128] = np.eye(128, dtype=np.float32)
    return c


def host_cmask():
    p = np.arange(128)
    i = np.arange(512)
    c = np.zeros((128, 2048), np.float32)
    for r in range(4):
        c[:, r * 512:(r + 1) * 512] = ((r * 128 + p)[:, None] <= i[None, :]).astype(np.float32)
    return c


def mix_consts(P, cst, flag, sm_c):
    S, st = P.S, P.stack
    nl = len(P.layers)
    P.cst = P.sb(st, "cst", [128, NCONST], F32)
    P.flag = P.sb(st, "flag", [128, 1], F32)
    P.smc = P.sb(st, "smc", [128, nl, 16], F32)
    P.ident = P.sb(st, "ident", [128, 128], BF16)
    P.flagmat = P.sb(st, "flagmat", [128, 128], BF16)
    P.b_cst = S.buf("cst")
    S.dma("sp", P.cst[:], cst, writes=[P.b_cst])
    S.dma("sp", P.flag[:], flag, writes=[P.b_cst], sem=P.b_cst)
    for li in range(nl):
        S.dma("sp", P.smc[:, li, :], sm_c[li], writes=[P.b_cst], sem=P.b_cst)
    S.op("dve", lambda e: e.tensor_copy(out=P.ident[:], in_=P.cst[:, C_ID:C_ID + 128]), reads=[P.b_cst], writes=[P.b_ones])
    S.op("dve", lambda e: e.tensor_scalar(out=P.flagmat[:], in0=P.cst[:, C_ID:C_ID + 128], scalar1=0.0, scalar2=P.flag[:, 0:1],
                                          op0=ALU.mult, op1=ALU.add), reads=[P.b_cst], writes=[P.b_ones])


def rope_tables(P, pos):
    S, st, T = P.S, P.stack, P.T
    P.cos2 = P.sb(st, "cos2", [128, T], F32)
    P.sin2 = P.sb(st, "sin2", [128, T], F32)
    P.b_rope = S.buf("rope")
    C1 = 6.28125
    C2 = 2 * np.pi - 6.28125
    with contextlib.ExitStack() as s2:
        posi = P.sb(s2, "posi", [128, T], I32)
        ang = P.sb(s2, "ang", [128, T], F32)
        r = P.sb(s2, "rr_", [128, T], F32)
        m = P.sb(s2, "mm_", [128, T], F32)
        b_pos, b_ang, b_r, b_m = S.buf("posi"), S.buf("ang"), S.buf("r"), S.buf("m")
        dv = lambda fn, rd, wr: S.op("dve", fn, reads=rd, writes=wr)
        S.dma("sp", posi[:], pos.partition_broadcast(128), writes=[b_pos])
        dv(lambda e: e.tensor_copy(out=ang[:], in_=posi[:]), [b_pos], [b_ang])
        dv(lambda e: e.tensor_scalar(out=ang[:], in0=ang[:], scalar1=P.cst[:, C_INV:C_INV + 1], scalar2=None, op0=ALU.mult),
           [P.b_cst], [b_ang])
        dv(lambda e: e.tensor_scalar(out=m[:], in0=ang[:], scalar1=float(1.0 / (2 * np.pi)), scalar2=None, op0=ALU.mult), [b_ang], [b_m])
        dv(lambda e: e.tensor_copy(out=posi[:], in_=m[:]), [b_m], [b_pos])
        dv(lambda e: e.tensor_copy(out=m[:], in_=posi[:]), [b_pos], [b_m])
        dv(lambda e: e.scalar_tensor_tensor(out=r[:], in0=m[:], scalar=-C1, in1=ang[:], op0=ALU.mult, op1=ALU.add), [b_m, b_ang], [b_r])
        dv(lambda e: e.scalar_tensor_tensor(out=r[:], in0=m[:], scalar=-C2, in1=r[:], op0=ALU.mult, op1=ALU.add), [b_m], [b_r])

        def wrap_sin(dst, after=None):
            dv(lambda e: e.tensor_scalar(out=m[:], in0=r[:], scalar1=PI, scalar2=-2 * PI, op0=ALU.is_gt, op1=ALU.mult), [b_r], [b_m])
            dv(lambda e: e.tensor_tensor(out=r[:], in0=r[:], in1=m[:], op=ALU.add), [b_m], [b_r])
            dv(lambda e: e.tensor_scalar(out=m[:], in0=r[:], scalar1=-PI, scalar2=PI, op0=ALU.max, op1=ALU.min), [b_r], [b_m])
            S.op("act", lambda e: e.activation(out=dst[:], in_=m[:], func=AF.Sin), reads=[b_m], writes=[P.b_rope])

        wrap_sin(P.sin2)
        dv(lambda e: e.tensor_scalar(out=P.sin2[:], in0=P.sin2[:], scalar1=P.cst[:, C_SIGN:C_SIGN + 1], scalar2=None, op0=ALU.mult),
           [P.b_cst], [P.b_rope])
        dv(lambda e: e.tensor_scalar(out=r[:], in0=r[:], scalar1=PI / 2, scalar2=None, op0=ALU.add), [], [b_r])
        wrap_sin(P.cos2)
        S.barrier()


def norm_block(P, li, k, xsrc, t0, hT, b_h, xc, b_xc, sq, b_sq, tmp, b_tmp, pst, b_pst, rstd, b_rstd):
    S = P.S
    gsl = P.gs[:, li, k * 16:(k + 1) * 16]
    shl = P.mc[:, li, (3 * k) * 16:(3 * k + 1) * 16]
    for c in range(NCH):
        s = c % 2
        S.dma("sp", xc[s][:], xsrc[c * 128:(c + 1) * 128, t0:t0 + TB], writes=[b_xc[s]])
        S.op("act", lambda e: e.activation(out=sq[s][:], in_=xc[s][:], func=AF.Square), reads=[b_xc[s]], writes=[b_sq[s]])
        S.mm([(pst[:], P.ones_bf[:], sq[s][:], c == 0, c == NCH - 1)], reads=[b_sq[s], P.b_ones], writes=[b_pst])
    S.op("act", lambda e: e.activation(out=rstd[:], in_=pst[:], func=AF.Sqrt, scale=1.0 / D, bias=P.eps_col[:]),
         reads=[b_pst], writes=[b_rstd])
    S.op("dve", lambda e: e.reciprocal(out=rstd[:], in_=rstd[:]), reads=[b_rstd], writes=[b_rstd])
    for c in range(NCH):
        s = c % 2
        S.dma("sp", xc[s][:], xsrc[c * 128:(c + 1) * 128, t0:t0 + TB], writes=[b_xc[s]])
        S.op("dve", lambda e: e.scalar_tensor_tensor(out=tmp[s][:], in0=xc[s][:], scalar=gsl[:, c:c + 1], in1=rstd[:],
                                                     op0=ALU.mult, op1=ALU.mult),
             reads=[b_xc[s], b_rstd, P.b_mod], writes=[b_tmp[s]])
        S.op("act", lambda e: e.activation(out=hT[:, c, :], in_=tmp[s][:], func=AF.Identity, bias=shl[:, c:c + 1], scale=1.0),
             reads=[b_tmp[s], P.b_mod], writes=[b_h])


def mixer(P, li, xs, w_inF, w_inV, w_uq_r, w_ukv_r, wo_r, D_, group):
    S, nc, T = P.S, P.nc, P.T
    NT = T // 128
    nblk = P.nblk
    cst = P.cst
    with contextlib.ExitStack() as mst:
        qT = P.sb(mst, "qT", [128, 4, T], BF16)
        kT = P.sb(mst, "kT", [128, 4, T], BF16)
        kd_tm = P.sb(mst, "kd_tm", [128, NT, 512], BF16)
        v_tm = P.sb(mst, "v_tm", [128, NT, 1024], BF16)
        b_qT, b_kT = S.bufs("qT", nblk), S.bufs("kT", nblk)
        b_kd, b_v = S.bufs("kd", NT), S.bufs("v", NT)
        with contextlib.ExitStack() as st:
            hTs = [P.sb(st, f"hT{i}", [128, NCH, TB], BF16) for i in range(2)]
            xc = [P.sb(st, f"xc{i}", [128, TB], F32) for i in range(2)]
            sq = [P.sb(st, f"sq{i}", [128, TB], BF16) for i in range(2)]
            tmp = [P.sb(st, f"tmp{i}", [128, TB], F32) for i in range(3)]
            rstd = P.sb(st, "rstd", [128, TB], F32)
            rstdx = [P.sb(st, f"rstdx{i}", [128, TB], F32) for i in range(2)]
            wF = [P.sb(st, f"wF{i}", [128, NCH, 128], BF16) for i in range(4)]
            wV = [P.sb(st, f"wV{i}", [128, NCH, 256], BF16) for i in range(2)]
            cq32 = P.sb(st, "cq32", [128, 4, TB], F32)
            stg = [P.sb(st, f"stg{i}", [128, 4, TB], BF16) for i in range(2)]
            pst = P.ps(st, "pst", [128, 512])
            pa = [P.ps(st, f"pa{i}", [128, 512]) for i in range(2)]
            pb = [P.ps(st, f"pb{i}", [128, 512]) for i in range(2)]
            pv = [P.ps(st, f"pv{i}", [128, 512]) for i in range(2)]
            pt = P.ps(st, "pt", [128, 1024], BF16)
            b_hs, b_rstd, b_pst, b_cq32, b_pt = S.bufs("h", 2), S.buf("rstd"), S.buf("pst"), S.buf("cq32"), S.buf("pt")
            b_rstdx = S.bufs("rstdx", 2)
            b_xc, b_sq, b_tmp = S.bufs("xc", 2), S.bufs("sq", 2), S.bufs("tmp", 3)
            gsl_ = P.gs[:, li, 16:32]
            shl_ = P.mc[:, li, 3 * 16:4 * 16]
            npc = [0]

            def pro_load(b, c):
                s_ = npc[0] % 2
                npc[0] += 1
                S.dma("sp", xc[s_][:], xs[c * 128:(c + 1) * 128, b * TB:(b + 1) * TB], writes=[b_xc[s_]])
                return s_

            def pro_a(b, c, s_):
                S.op("act", lambda e: e.activation(out=sq[s_][:], in_=xc[s_][:], func=AF.Square), reads=[b_xc[s_]], writes=[b_sq[s_]])
                S.mm([(pst[:], P.ones_bf[:], sq[s_][:], c == 0, c == NCH - 1)], reads=[b_sq[s_], P.b_ones], writes=[b_pst])

            def pro_r(b):
                nb = b % 2
                S.op("act", lambda e: e.activation(out=rstdx[nb][:], in_=pst[:], func=AF.Sqrt, scale=1.0 / D, bias=P.eps_col[:]),
                     reads=[b_pst], writes=[b_rstdx[nb]])
                S.op("dve", lambda e: e.reciprocal(out=rstdx[nb][:], in_=rstdx[nb][:]), reads=[b_rstdx[nb]], writes=[b_rstdx[nb]])

            def pro_b(b, c, s_):
                nb = b % 2
                S.op("dve", lambda e: e.scalar_tensor_tensor(out=tmp[s_][:], in0=xc[s_][:], scalar=gsl_[:, c:c + 1], in1=rstdx[nb][:],
                                                             op0=ALU.mult, op1=ALU.mult),
                     reads=[b_xc[s_], b_rstdx[nb], P.b_mod], writes=[b_tmp[s_]])
                S.op("act", lambda e: e.activation(out=hTs[nb][:, c, :], in_=tmp[s_][:], func=AF.Identity, bias=shl_[:, c:c + 1], scale=1.0),
                     reads=[b_tmp[s_], P.b_mod], writes=[b_hs[nb]])
            b_wF, b_wV, b_stg = S.bufs("wF", 4), S.bufs("wV", 2), S.bufs("stg", 2)
            b_pa, b_pb, b_pv = S.bufs("pa", 2), S.bufs("pb", 2), S.bufs("pv", 2)
            loads = []
            for b in range(nblk):
                for sl in range(NSLAB):
                    for ch in range(2):
                        n = (b * NSLAB + sl) * 2 + ch
                        loads.append((b_wF[n % 4], wF[n % 4][:], w_inF[sl][:, ch], "wF", 4))
                for vs in range(4):
                    n = b * 4 + vs
                    loads.append((b_wV[n % 2], wV[n % 2][:], w_inV[vs], "wV", 2))
            strm = Stream(S, "pool", loads)
            step = 0
            nstg = 0
            strm.ensure(3)
            for c in range(NCH):
                pro_a(0, c, pro_load(0, c))
            pro_r(0)
            for c in range(NCH):
                pro_b(0, c, pro_load(0, c))
            for b in range(nblk):
                t0 = b * TB
                tsl = slice(t0, t0 + TB)
                hT, b_h = hTs[b % 2], b_hs[b % 2]
                strm.ensure(step + 3)
                for sl in range(NSLAB):
                    strm.ensure(step + 3)
                    n0 = (b * NSLAB + sl) * 2
                    s = sl % 2
                    pcs = []
                    if b + 1 < nblk:
                        cc_ = (2 * sl, 2 * sl + 1) if sl < 8 else (2 * (sl - 8), 2 * (sl - 8) + 1)
                        pcs = [(c, pro_load(b + 1, c)) for c in cc_]
                    S.mm([(pa[s][:], wF[n0 % 4][:, c, :], hT[:, c, :], c == 0, c == NCH - 1) for c in range(NCH)],
                         reads=[b_wF[n0 % 4], b_h], writes=[b_pa[s]])
                    strm.consumed("wF")
                    step += 1
                    strm.ensure(step + 3)
                    S.mm([(pb[s][:], wF[(n0 + 1) % 4][:, c, :], hT[:, c, :], c == 0, c == NCH - 1) for c in range(NCH)],
                         reads=[b_wF[(n0 + 1) % 4], b_h], writes=[b_pb[s]])
                    strm.consumed("wF")
                    step += 1
                    if b + 1 < nblk:
                        if sl == 8:
                            pro_r(b + 1)
                        for c, ps_ in pcs:
                            (pro_a if sl < 8 else pro_b)(b + 1, c, ps_)
                    if sl < 8:
                        hp = sl % 4
                        dst, bd = (qT, b_qT) if sl < 4 else (kT, b_kT)
                        S.op("dve", lambda e: e.tensor_tensor(out=tmp[0][:], in0=pa[s][:], in1=P.cos2[:, tsl], op=ALU.mult),
                             reads=[b_pa[s], P.b_rope], writes=[b_tmp[0]])
                        S.op("dve", lambda e: e.tensor_tensor(out=tmp[1][:], in0=pb[s][:], in1=P.sin2[:, tsl], op=ALU.mult),
                             reads=[b_pb[s], P.b_rope], writes=[b_tmp[1]])
                        S.op("dve", lambda e: e.tensor_tensor(out=dst[:, hp, tsl], in0=tmp[0][:], in1=tmp[1][:], op=ALU.add),
                             reads=[b_tmp[0], b_tmp[1]], writes=[bd[b]])
                    elif sl < 12:
                        for which, pp, bp in ((0, pa[s], b_pa[s]), (1, pb[s], b_pb[s])):
                            h = (sl - 8) * 2 + which
                            sb_ = stg[nstg % 2]
                            S.op("act", lambda e: e.activation(out=tmp[2][:], in_=pp[:], func=AF.Silu), reads=[bp], writes=[b_tmp[2]])
                            S.op("dve", lambda e: e.tensor_scalar(out=sb_[:, 0, :], in0=tmp[2][:], scalar1=P.smc[:, li, h:h + 1],
                                                                  scalar2=None, op0=ALU.mult),
                                 reads=[b_tmp[2], P.b_cst], writes=[b_stg[nstg % 2]])
                            S.dma("sp", D_["gsd"][h * 128:(h + 1) * 128, tsl], sb_[:, 0, :], reads=[b_stg[nstg % 2]], sem=b_stg[nstg % 2])
                            nstg += 1
                    elif sl < 15:
                        base = (sl - 12) * 2 if sl < 14 else 0
                        nchk = 4 if sl < 14 else 2
                        for which, pp, bp in ((0, pa[s], b_pa[s]), (1, pb[s], b_pb[s])):
                            ci = base + which
                            first = (ci == 0)
                            last = (ci == nchk - 1)
                            S.op("dve", lambda e: e.tensor_copy(out=cq32[:, ci, :], in_=pp[:]), reads=[bp], writes=[b_cq32])
                            S.op("act", lambda e: e.activation(out=sq[which][:], in_=pp[:], func=AF.Square), reads=[bp], writes=[b_sq[which]])
                            S.mm([(pst[:], P.ones_bf[:], sq[which][:], first, last)], reads=[b_sq[which], P.b_ones], writes=[b_pst])
                        if sl == 13 or sl == 14:
                            S.op("act", lambda e: e.activation(out=rstd[:], in_=pst[:], func=AF.Sqrt, scale=1.0 / (128 * nchk), bias=P.eps_col[:]),
                                 reads=[b_pst], writes=[b_rstd])
                            S.op("dve", lambda e: e.reciprocal(out=rstd[:], in_=rstd[:]), reads=[b_rstd], writes=[b_rstd])
                            sb_ = stg[nstg % 2]
                            for ci in range(nchk):
                                S.op("dve", lambda e: e.tensor_tensor(out=sb_[:, ci, :], in0=cq32[:, ci, :], in1=rstd[:], op=ALU.mult),
                                     reads=[b_cq32, b_rstd], writes=[b_stg[nstg % 2]])
                            dd = D_["cqs"] if sl == 13 else D_["xkv"]
                            S.dma("sp", dd[:, tsl].rearrange("(c p) t -> p c t", p=128), sb_[:, 0:nchk, :],
                                  reads=[b_stg[nstg % 2]], sem=b_stg[nstg % 2])
                            nstg += 1
                    else:
                        sb_ = stg[nstg % 2]
                        S.op("dve", lambda e: e.tensor_tensor(out=tmp[0][:], in0=pa[s][:], in1=P.cos2[:, tsl], op=ALU.mult),
                             reads=[b_pa[s], P.b_rope], writes=[b_tmp[0]])
                        S.op("dve", lambda e: e.tensor_tensor(out=tmp[1][:], in0=pb[s][:], in1=P.sin2[:, tsl], op=ALU.mult),
                             reads=[b_pb[s], P.b_rope], writes=[b_tmp[1]])
                        S.op("dve", lambda e: e.tensor_tensor(out=sb_[:, 0, :], in0=tmp[0][:], in1=tmp[1][:], op=ALU.add),
                             reads=[b_tmp[0], b_tmp[1]], writes=[b_stg[nstg % 2]])
                        S.dma("sp", D_["xpe"][:, tsl], sb_[0:64, 0, :], reads=[b_stg[nstg % 2]], sem=b_stg[nstg % 2])
                        nstg += 1
                for tt in range(4):
                    tile_i = b * 4 + tt
                    half = tt % 2
                    S.pe_ops([(lambda e, hp=hp: e.transpose(pt[:, half * 512 + hp * 128: half * 512 + (hp + 1) * 128],
                                                           kT[:, hp, t0 + tt * 128: t0 + (tt + 1) * 128], P.ident[:]))
                              for hp in range(4)], reads=[b_kT[b], P.b_ones], writes=[b_pt])
                    S.op("dve", lambda e: e.tensor_tensor(out=kd_tm[:, tile_i, :], in0=pt[:, half * 512:(half + 1) * 512],
                                                          in1=cst[:, C_KDEC:C_KDEC + 512], op=ALU.mult),
                         reads=[b_pt, P.b_cst], writes=[b_kd[tile_i]])
                for vs in range(4):
                    strm.ensure(step + 3)
                    n = b * 4 + vs
                    w = wV[n % 2]
                    for tt in range(4):
                        tile_i = b * 4 + tt
                        s = tt % 2
                        S.mm([(pv[s][:, 0:256], hT[:, c, tt * 128:(tt + 1) * 128], w[:, c, :], c == 0, c == NCH - 1) for c in range(NCH)],
                             reads=[b_wV[n % 2], b_h], writes=[b_pv[s]])
                        S.op("act", lambda e: e.activation(out=v_tm[:, tile_i, vs * 256:(vs + 1) * 256], in_=pv[s][:, 0:256], func=AF.Copy),
                             reads=[b_pv[s]], writes=[b_v[tile_i]])
                    strm.consumed("wV")
                    step += 1
            S.barrier()
            if getattr(P, "mix_stop", None) == "P":
                return
        with contextlib.ExitStack() as st:
            NTc = NT
            state = P.sb(st, "state", [128, 4, 256], F32)
            states_bf = [P.sb(st, f"states_bf{i}", [128, NTc + 1, 256], BF16) for i in range(2)]
            stload = P.sb(st, "stload", [128, 4, 256], BF16)
            SmT = [P.sb(st, f"SmT{i}", [128, 256], BF16) for i in range(3)]
            qd = [P.sb(st, f"qd{i}", [128, 128], BF16) for i in range(3)]
            sqr = [P.sb(st, f"sqr{i}", [128, TB], BF16) for i in range(2)]
            gbuf = [P.sb(st, f"gbuf{i}", [128, TB], BF16) for i in range(2)]
            rr = [P.sb(st, f"rr{i}", [128, TB], F32) for i in range(2)]
            tt_ = [P.sb(st, f"tr{i}", [128, TB], F32) for i in range(2)]
            ystg = [P.sb(st, f"ystg{i}", [128, TB], BF16) for i in range(2)]
            pS = [[P.ps(st, f"pS{x}{i}", [128, 512]) for i in range(2)] for x in range(2)]
            po = [P.ps(st, f"po{x}", [128, 512]) for x in range(2)]
            pinc = P.ps(st, "pinc", [128, 512])
            pss = P.ps(st, "pss", [128, 512])
            b_stl, b_pinc, b_pss = S.buf("stl"), S.buf("pinc"), S.buf("pss")
            b_SmT, b_qd = S.bufs("SmT", 3), S.bufs("qd", 3)
            b_sqr, b_g, b_rr, b_tr, b_ystg = (S.bufs(n, 2) for n in ("sqr", "g", "rr", "tr", "ystg"))
            b_pS = [S.bufs(f"pS{x}", 2) for x in range(2)]
            b_po = S.bufs("po", 2)
            b_st4 = S.bufs("st4", 4)
            b_sts = [S.bufs(f"sts{i}_", NTc + 1) for i in range(2)]

            def state_init(hp, zero):
                sb = hp % 2
                if zero:
                    S.op("dve", lambda e: e.memset(state[:, hp, :], 0.0), writes=[b_st4[hp]])
                else:
                    S.op("dve", lambda e: e.tensor_scalar(out=state[:, hp, :], in0=stload[:, hp, :], scalar1=P.flag[:, 0:1],
                                                          scalar2=None, op0=ALU.mult),
                         reads=[b_stl, P.b_cst], writes=[b_st4[hp]])
                S.op("act", lambda e: e.activation(out=states_bf[sb][:, 0, :], in_=state[:, hp, :], func=AF.Copy),
                     reads=[b_st4[hp]], writes=[b_sts[sb][0]])

            def state_step(hp, n):
                sb = hp % 2
                S.mm([(pinc[:, 0:256], kd_tm[:, n, hp * 128:(hp + 1) * 128], v_tm[:, n, hp * 256:(hp + 1) * 256], True, True)],
                     reads=[b_kd[n], b_v[n]], writes=[b_pinc])
                S.op("dve", lambda e: e.scalar_tensor_tensor(out=state[:, hp, :], in0=state[:, hp, :], scalar=cst[:, C_CD + hp:C_CD + hp + 1],
                                                             in1=pinc[:, 0:256], op0=ALU.mult, op1=ALU.add),
                     reads=[b_pinc, P.b_cst], writes=[b_st4[hp]])
                S.op("act", lambda e: e.activation(out=states_bf[sb][:, n + 1, :], in_=state[:, hp, :], func=AF.Copy),
                     reads=[b_st4[hp]], writes=[b_sts[sb][n + 1]])

            xfin = P.sb(st, "xfin", [128, 4, 256], BF16)
            b_xfin = S.buf("xfin")
            for hp in range(4):
                S.op("dve", lambda e: e.memset(state[:, hp, :], 0.0), writes=[b_st4[hp]])
            pincs = [pinc, pss]
            b_pincs = [b_pinc, b_pss]
            for n in range(NTc):
                for hp in range(4):
                    pi = (n * 4 + hp) % 2
                    S.mm([(pincs[pi][:, 0:256], kd_tm[:, n, hp * 128:(hp + 1) * 128], v_tm[:, n, hp * 256:(hp + 1) * 256], True, True)],
                         reads=[b_kd[n], b_v[n]], writes=[b_pincs[pi]])
                    S.op("dve", lambda e: e.scalar_tensor_tensor(out=state[:, hp, :], in0=state[:, hp, :], scalar=cst[:, C_CD + hp:C_CD + hp + 1],
                                                                 in1=pincs[pi][:, 0:256], op0=ALU.mult, op1=ALU.add),
                         reads=[b_pincs[pi], P.b_cst], writes=[b_st4[hp]])
            for hp in range(4):
                S.op("act", lambda e: e.activation(out=xfin[:, hp, :], in_=state[:, hp, :], func=AF.Copy), reads=[b_st4[hp]], writes=[b_xfin])
            S.dma("sp", D_["xst"].rearrange("(h p) e -> p h e", p=128), xfin[:], reads=[b_xfin], sem=b_xfin)
            S.barrier()
            if getattr(P, "mix_stop", None) == "R1":
                return
            exchange(P, D_, group)
            if getattr(P, "mix_stop", None) == "X":
                return
            S.dma("sp", stload[:], D_["gst"][0:512, :].rearrange("(h p) e -> p h e", p=128), writes=[b_stl])

            state_init(0, False)
            for n in range(NTc):
                state_step(0, n)
            cnt = 0
            ny = 0
            for hp in range(4):
                sb = hp % 2
                if hp + 1 < 4:
                    state_init(hp + 1, False)

                def emit_S(n):
                    tsl = slice(n * 128, (n + 1) * 128)
                    blk = n // 4
                    for x in range(2):
                        rows = slice(x * 64, x * 64 + 64)
                        S.mm([(pS[x][n % 2][:, 0:128], kT[rows, hp, tsl], qT[rows, hp, tsl], True, True)],
                             reads=[b_kT[blk], b_qT[blk]], writes=[b_pS[x][n % 2]])

                emit_S(0)
                for n in range(NTc):
                    blk, n4 = divmod(n, 4)
                    tsl = slice(n * 128, (n + 1) * 128)
                    s = cnt % 3
                    cnt += 1
                    if n + 1 < NTc:
                        emit_S(n + 1)
                    for x in range(2):
                        S.op("dve", lambda e: e.tensor_tensor(out=SmT[s][:, x * 128:(x + 1) * 128], in0=pS[x][n % 2][:, 0:128],
                                                              in1=cst[:, C_MASK + hp * 256 + x * 128: C_MASK + hp * 256 + (x + 1) * 128], op=ALU.mult),
                             reads=[b_pS[x][n % 2], P.b_cst], writes=[b_SmT[s]])
                    S.op("pool", lambda e: e.tensor_tensor(out=qd[s][:], in0=qT[:, hp, tsl],
                                                          in1=cst[:, C_QDEC + hp * 128: C_QDEC + (hp + 1) * 128], op=ALU.mult),
                         reads=[b_qT[blk], P.b_cst], writes=[b_qd[s]])
                    for x in range(2):
                        h = 2 * hp + x
                        rows = slice(x * 64, x * 64 + 64)
                        S.mm([(po[x][:, n4 * 128:(n4 + 1) * 128], v_tm[:, n, h * 128:(h + 1) * 128], SmT[s][:, x * 128:(x + 1) * 128], True, False),
                              (po[x][:, n4 * 128:(n4 + 1) * 128], states_bf[sb][rows, n, x * 128:(x + 1) * 128], qd[s][rows, :], False, True)],
                             reads=[b_v[n], b_SmT[s], b_sts[sb][n], b_qd[s]], writes=[b_po[x]])
                    if hp + 1 < 4:
                        state_step(hp + 1, n)
                    if n4 == 3:
                        t0 = blk * TB
                        for x in range(2):
                            h = 2 * hp + x
                            s2 = ny % 2
                            ny += 1
                            S.dma("sp", gbuf[s2][:], D_["gsd"][h * 128:(h + 1) * 128, t0:t0 + TB], writes=[b_g[s2]])
                            S.op("act", lambda e: e.activation(out=sqr[s2][:], in_=po[x][:], func=AF.Square),
                                 reads=[b_po[x]], writes=[b_sqr[s2]])
                            S.op("act", lambda e: e.activation(out=tt_[s2][:], in_=po[x][:], func=AF.Copy),
                                 reads=[b_po[x]], writes=[b_tr[s2]])
                            S.mm([(pss[:], P.ones_bf[:], sqr[s2][:], True, True)], reads=[b_sqr[s2], P.b_ones], writes=[b_pss])
                            S.op("act", lambda e: e.activation(out=rr[s2][:], in_=pss[:], func=AF.Sqrt, scale=1.0 / 128, bias=P.eps_col[:]),
                                 reads=[b_pss], writes=[b_rr[s2]])
                            S.op("dve", lambda e: e.reciprocal(out=rr[s2][:], in_=rr[s2][:]), reads=[b_rr[s2]], writes=[b_rr[s2]])
                            S.op("dve", lambda e: e.tensor_tensor(out=tt_[s2][:], in0=tt_[s2][:], in1=rr[s2][:], op=ALU.mult),
                                 reads=[b_rr[s2]], writes=[b_tr[s2]])
                            S.op("dve", lambda e: e.tensor_tensor(out=ystg[s2][:], in0=tt_[s2][:], in1=gbuf[s2][:], op=ALU.mult),
                                 reads=[b_tr[s2], b_g[s2]], writes=[b_ystg[s2]])
                            S.dma("sp", D_["ys"][h * 128:(h + 1) * 128, t0:t0 + TB], ystg[s2][:], reads=[b_ystg[s2]], sem=b_ystg[s2])
            S.barrier()
            if getattr(P, "mix_stop", None) == "R":
                return
    with contextlib.ExitStack() as st:
        cqn = P.sb(st, "cqn", [128, 4, T], BF16)
        ckv = P.sb(st, "ckv", [128, 2, 2 * T], BF16)
        kpe = P.sb(st, "kpe", [64, 2 * T], BF16)
        wuq = P.sb(st, "wuq", [128, 4, 1536], BF16)
        wuq_sw = P.sb(st, "wuq_sw", [128, 4, 8, 64], BF16)
        wukv = P.sb(st, "wukv", [128, 2, 2048], BF16)
        wst = P.sb(st, "wst", [128, 2048], F32)
        gcol = P.sb(st, "gcol", [128, 4], F32)
        knT = [P.sb(st, f"knT{i}", [128, 2 * T], BF16) for i in range(2)]
        vh = [P.sb(st, f"vh{i}", [128, 2 * NT, 128], BF16) for i in range(2)]
        qn = [P.sb(st, f"qn{i}", [128, TB], BF16) for i in range(2)]
        qpe = [P.sb(st, f"qpe{i}", [64, TB], BF16) for i in range(2)]
        pT = [P.sb(st, f"pT{i}", [128, TB], BF16) for i in range(3)]
        t1 = [P.sb(st, f"t1{i}", [64, TB], F32) for i in range(2)]
        rden = [P.sb(st, f"rden{i}", [128, TB], F32) for i in range(2)]
        ystg = [P.sb(st, f"ystga{i}", [128, TB], BF16) for i in range(2)]
        pS = [P.ps(st, f"aS{i}", [128, 512]) for i in range(2)]
        py = [P.ps(st, f"ay{i}", [128, 512]) for i in range(2)]
        pden = [P.ps(st, f"ad{i}", [128, 512]) for i in range(2)]
        pk = [P.ps(st, f"ak{i}", [128, 512]) for i in range(2)]
        b_cqn, b_ckv, b_kpe, b_wuq, b_wukv, b_wst, b_gcol = (S.buf(n) for n in ("cqn", "ckv", "kpe", "wuq", "wukv", "wst", "gcol"))
        b_knT, b_vh, b_rden = S.bufs("knT", 2), S.bufs("vh", 2), S.bufs("rden", 2)
        b_qn, b_qpe, b_t1, b_ystg = S.bufs("qn", 2), S.bufs("qpe", 2), S.bufs("t1", 2), S.bufs("ystga", 2)
        b_pT = S.bufs("pT", 3)
        b_pS, b_py, b_pden, b_pk = S.bufs("aS", 2), S.bufs("ay", 2), S.bufs("ad", 2), S.bufs("ak", 2)
        cmt = P.sb(st, "cmt", [128, 2048], BF16)
        b_cmt = S.buf("cmt")
        S.dma("pool", cmt[:], D_["cmask"], writes=[b_cmt])
        S.dma("sp", cqn[:], D_["cqs"].rearrange("(c p) t -> p c t", p=128), writes=[b_cqn])
        S.dma("sp", ckv[:, :, 0:T], D_["gkv"][0:256, :].rearrange("(c p) t -> p c t", p=128), writes=[b_ckv], sem=b_ckv)
        S.dma("sp", ckv[:, :, T:2 * T], D_["xkv"].rearrange("(c p) t -> p c t", p=128), writes=[b_ckv], sem=b_ckv)
        S.dma("sp", kpe[:, 0:T], D_["gpe"][0:64, :], writes=[b_kpe], sem=b_kpe)
        S.dma("sp", kpe[:, T:2 * T], D_["xpe"], writes=[b_kpe], sem=b_kpe)
        S.op("dve", lambda e: e.tensor_scalar(out=gcol[:], in0=P.smc[:, li, 8:12], scalar1=SM_SCALE, scalar2=None, op0=ALU.mult),
             reads=[P.b_cst], writes=[b_gcol])
        for kc in range(4):
            S.dma("sp", wst[:, 0:1536], w_uq_r[kc], writes=[b_wst])
            S.op("dve", lambda e: e.tensor_scalar(out=wuq[:, kc, :], in0=wst[:, 0:1536], scalar1=gcol[:, kc:kc + 1], scalar2=None, op0=ALU.mult),
                 reads=[b_wst, b_gcol], writes=[b_wuq])
        for kc in range(2):
            S.dma("sp", wst[:, :], w_ukv_r[kc], writes=[b_wst])
            S.op("dve", lambda e: e.tensor_scalar(out=wukv[:, kc, :], in0=wst[:, :], scalar1=P.smc[:, li, 12 + kc:13 + kc], scalar2=None, op0=ALU.mult),
                 reads=[b_wst, P.b_cst], writes=[b_wukv])
        wv = wuq[:, :, :].rearrange("p k (h c) -> p k h c", h=8)
        S.op("dve", lambda e: e.tensor_copy(out=wuq_sw[:, :, :, 0:32], in_=wv[:, :, :, 160:192]), reads=[b_wuq], writes=[b_wuq])
        S.op("dve", lambda e: e.tensor_copy(out=wuq_sw[:, :, :, 32:64], in_=wv[:, :, :, 128:160]), reads=[b_wuq], writes=[b_wuq])
        bg = []

        def kv_tasks(h):
            kb_ = h % 2
            tasks = []
            for kb in range(2 * T // 512):
                def f(kb=kb):
                    s_ = kb % 2
                    S.mm([(pk[s_][:], wukv[:, kc, h * 256: h * 256 + 128], ckv[:, kc, kb * 512:(kb + 1) * 512], kc == 0, kc == 1) for kc in range(2)],
                         reads=[b_wukv, b_ckv], writes=[b_pk[s_]])
                    S.op("dve", lambda e: e.tensor_copy(out=knT[kb_][:, kb * 512:(kb + 1) * 512], in_=pk[s_][:]), reads=[b_pk[s_]], writes=[b_knT[kb_]])
                tasks.append(f)
            for g4 in range(2 * NT // 4):
                def f(g4=g4):
                    s_ = g4 % 2
                    for t4 in range(4):
                        jt = g4 * 4 + t4
                        S.mm([(pk[s_][:, t4 * 128:(t4 + 1) * 128], ckv[:, kc, jt * 128:(jt + 1) * 128], wukv[:, kc, h * 256 + 128: h * 256 + 256], kc == 0, kc == 1)
                              for kc in range(2)], reads=[b_wukv, b_ckv], writes=[b_pk[s_]])
                    dstv = vh[kb_][:, g4 * 4:(g4 + 1) * 4, :].rearrange("p a b -> p (a b)")
                    if g4 * 4 < NT:
                        S.op("dve", lambda e: e.tensor_scalar(out=dstv, in0=pk[s_][:], scalar1=P.flag[:, 0:1], scalar2=None, op0=ALU.mult),
                             reads=[b_pk[s_], P.b_cst], writes=[b_vh[kb_]])
                    else:
                        S.op("dve", lambda e: e.tensor_copy(out=dstv, in_=pk[s_][:]), reads=[b_pk[s_]], writes=[b_vh[kb_]])
                tasks.append(f)
            return tasks

        def q_tasks(h, ib, s):
            qsl = slice(ib * TB, (ib + 1) * TB)

            def f1():
                S.mm([(pk[0][:], wuq[:, kc, h * 192: h * 192 + 128], cqn[:, kc, qsl], kc == 0, kc == 3) for kc in range(4)],
                     reads=[b_wuq, b_cqn], writes=[b_pk[0]])
                S.op("dve", lambda e: e.tensor_copy(out=qn[s][:], in_=pk[0][:]), reads=[b_pk[0]], writes=[b_qn[s]])

            def f2():
                S.mm([(pk[1][0:64, :], wuq[:, kc, h * 192 + 128: h * 192 + 192], cqn[:, kc, qsl], kc == 0, kc == 3) for kc in range(4)],
                     reads=[b_wuq, b_cqn], writes=[b_pk[1]])
                S.op("dve", lambda e: e.tensor_tensor(out=t1[0][:], in0=pk[1][0:64, :], in1=P.cos2[0:64, qsl], op=ALU.mult),
                     reads=[b_pk[1], P.b_rope], writes=[b_t1[0]])

            def f3():
                S.mm([(pk[1][0:64, :], wuq_sw[:, kc, h, :], cqn[:, kc, qsl], kc == 0, kc == 3) for kc in range(4)],
                     reads=[b_wuq, b_cqn], writes=[b_pk[1]])
                S.op("dve", lambda e: e.tensor_tensor(out=t1[1][:], in0=pk[1][0:64, :], in1=P.sin2[0:64, qsl], op=ALU.mult),
                     reads=[b_pk[1], P.b_rope], writes=[b_t1[1]])
                S.op("dve", lambda e: e.tensor_tensor(out=qpe[s][:], in0=t1[0][:], in1=t1[1][:], op=ALU.add),
                     reads=[b_t1[0], b_t1[1]], writes=[b_qpe[s]])
            return [f1, f2, f3]

        seq = [(h, ib) for h in range(8) for ib in range(nblk)]
        for f in kv_tasks(0) + q_tasks(0, 0, 0):
            f()
        nun = 0
        for idx, (h, ib) in enumerate(seq):
            kb_ = h % 2
            s = idx % 2
            q0 = ib * TB
            qsl = slice(q0, q0 + TB)
            if idx + 1 < len(seq):
                bg.extend(q_tasks(seq[idx + 1][0], seq[idx + 1][1], (idx + 1) % 2))
            if ib == 1 and h + 1 < 8:
                bg.extend(kv_tasks(h + 1))
            if True:
                tiles = list(range(NT)) + [NT + j for j in range((ib + 1) * 4)]
                ya = idx % 2
                ubase = nun
                nun += len(tiles)

                def emit_S(ti):
                    jt = tiles[ti]
                    u = (ubase + ti) % 2
                    ksl = slice(jt * 128, (jt + 1) * 128)
                    S.mm([(pS[u][:], knT[kb_][:, ksl], qn[s][:], True, False),
                          (pS[u][:], kpe[0:64, ksl], qpe[s][:], False, True)],
                         reads=[b_knT[kb_], b_kpe, b_qn[s], b_qpe[s]], writes=[b_pS[u]])

                emit_S(0)
                for ti, jt in enumerate(tiles):
                    u = (ubase + ti) % 2
                    pp = (ubase + ti) % 3
                    if ti + 1 < len(tiles):
                        emit_S(ti + 1)
                    if bg and ti >= 1:
                        bg.pop(0)()
                    S.op("act", lambda e: e.activation(out=pT[pp][:], in_=pS[u][:], func=AF.Exp), reads=[b_pS[u]], writes=[b_pT[pp]])
                    r = jt - NT - ib * 4
                    if r >= 0:
                        S.op("dve", lambda e: e.tensor_tensor(out=pT[pp][:], in0=pT[pp][:], in1=cmt[:, r * 512:(r + 1) * 512], op=ALU.mult),
                             reads=[b_cmt], writes=[b_pT[pp]])
                    first, last = ti == 0, ti == len(tiles) - 1
                    onesm = P.flagmat if jt < NT else P.ones_bf
                    S.mm([(py[ya][:], vh[kb_][:, jt, :], pT[pp][:], first, last),
                          (pden[ya][:], onesm[:], pT[pp][:], first, last)],
                         reads=[b_vh[kb_], b_pT[pp], P.b_ones], writes=[b_py[ya], b_pden[ya]])
                while bg and (len(bg) > 16 or ib == nblk - 1 or bg[0].__name__ != "f"):
                    bg.pop(0)()
                S.op("dve", lambda e: e.reciprocal(out=rden[ya][:], in_=pden[ya][:]), reads=[b_pden[ya]], writes=[b_rden[ya]])
                S.op("dve", lambda e: e.tensor_tensor(out=ystg[s][:], in0=py[ya][:], in1=rden[ya][:], op=ALU.mult),
                     reads=[b_py[ya], b_rden[ya]], writes=[b_ystg[s]])
                S.dma("sp", D_["ys"][(8 + h) * 128:(9 + h) * 128, qsl], ystg[s][:], reads=[b_ystg[s]], sem=b_ystg[s])
        S.barrier()
        if getattr(P, "mix_stop", None) == "A":
            return
    with contextlib.ExitStack() as st:
        TBO = 1024
        nbo = T // TBO
        yblk = [P.sb(st, f"yblk{i}", [128, NCH, TBO], BF16) for i in range(2)]
        wo = [P.sb(st, f"wo{i}", [128, NCH, 128], BF16) for i in range(4)]
        xc = [P.sb(st, f"oxc{i}", [128, TBO], F32) for i in range(3)]
        po = [P.ps(st, f"opo{i}", [128, 512]) for i in range(4)]
        b_y, b_wo, b_xc, b_po = S.bufs("yblk", 2), S.bufs("wo", 4), S.bufs("oxc", 3), S.bufs("opo", 4)
        loads = []
        for b in range(nbo):
            for m in range(NCH):
                n = b * NCH + m
                loads.append((b_wo[n % 4], wo[n % 4][:], wo_r[m], "wo", 4))
        strm = Stream(S, "pool", loads)
        g2 = P.gh[:, li, 16:32]
        step = 0

        def ld_x(n):
            if n >= nbo * NCH:
                return
            bb, mm_ = divmod(n, NCH)
            S.dma("sp", xc[n % 3][:], xs[mm_ * 128:(mm_ + 1) * 128, bb * TBO:(bb + 1) * TBO], writes=[b_xc[n % 3]])
        ld_x(0)
        for b in range(nbo):
            t0 = b * TBO
            S.dma("sp", yblk[b % 2][:], D_["ys"][:, t0:t0 + TBO].rearrange("(c p) t -> p c t", p=128), writes=[b_y[b % 2]])
            for m in range(NCH):
                strm.ensure(step + 3)
                n = b * NCH + m
                x3 = n % 3
                ld_x(n + 1)
                for th in range(2):
                    pi = (2 * n + th) % 4
                    tsl = slice(th * 512, (th + 1) * 512)
                    S.mm([(po[pi][:], wo[n % 4][:, kc, :], yblk[b % 2][:, kc, tsl], kc == 0, kc == NCH - 1) for kc in range(NCH)],
                         reads=[b_wo[n % 4], b_y[b % 2]], writes=[b_po[pi]])
                strm.consumed("wo")
                for th in range(2):
                    pi = (2 * n + th) % 4
                    tsl = slice(th * 512, (th + 1) * 512)
                    S.op("dve", lambda e: e.scalar_tensor_tensor(out=xc[x3][:, tsl], in0=po[pi][:], scalar=g2[:, m:m + 1], in1=xc[x3][:, tsl],
                                                                 op0=ALU.mult, op1=ALU.add),
                         reads=[b_po[pi], P.b_mod], writes=[b_xc[x3]])
                S.dma("sp", xs[m * 128:(m + 1) * 128, t0:t0 + TBO], xc[x3][:], reads=[b_xc[x3]], sem=b_xc[x3])
                step += 1
        S.barrier()


def exchange(P, D_, group):
    S = P.S
    g = P.nc.gpsimd
    if "cc" not in S.sems:
        S._newsem("cc")
    for src, dst in (("xkv", "gkv"), ("xpe", "gpe"), ("xst", "gst")):
        ins = g.collective_compute("AllGather", ALU.bypass, replica_groups=group, ins=[D_[src]], outs=[D_[dst]])
        S.cur["cc"] += 1
        ins.then_inc(S.sems["cc"], 1)
    S._wait("pool", [("cc", S.cur["cc"])])
    S.barrier()


def build(cfg):
    nc = bass.Bass("TRN2", target_bir_lowering=False)
    nl = cfg.get("nl", L)
    T = cfg.get("T", TOK)
    ncores = cfg.get("ncores", 8)
    stop = cfg.get("stop", None)
    A = {}

    def inp(name, shape, dt=F32):
        A[name] = nc.dram_tensor(name, shape, dt, kind="ExternalInput").ap()
        return A[name]

    xT = inp("xT", [D, T])
    ccol_all = inp("ccol_all", [128, 16, 4])
    onehot = inp("onehot", [128, 4])
    pos = inp("pos", [1, T], I32)
    flag = inp("flag", [128, 1])
    cst = inp("cst", [128, NCONST])
    ada_n = 4 if ncores == 8 else 2
    CW = NMOD * 128 // ada_n
    w_ada_r = inp("w_ada_r", [nl, 128, 16, CW])
    ada_in = nc.dram_tensor("ada_in", [nl * 128, (NMOD // ada_n) * 4], F32).ap()
    ada_g1 = nc.dram_tensor("ada_g1", [2 * nl * 128, (NMOD // ada_n) * 4], F32).ap()
    ada_g = ada_g1
    if ada_n == 4:
        ada_g = nc.dram_tensor("ada_g2", [4 * nl * 128, (NMOD // ada_n) * 4], F32).ap()
    b_ada_c = inp("b_ada_c", [nl, 128, NMOD])
    ng_c = inp("ng_c", [nl, 3, 128, 16])
    sm_c = inp("sm_c", [nl, 128, 16])
    fin_c = inp("fin_c", [128, 16])
    wgu_r = inp("wgu_r", [nl, 2, NFF, 128, 16, 256])
    wd_r = inp("wd_r", [nl, 2, NCH, 2, 128, NFF // 2, 128])
    w_inF = inp("w_inF", [nl, NSLAB, 128, 2, 16, 128])
    w_inV = inp("w_inV", [nl, 4, 128, 16, 256])
    w_uq_r = inp("w_uq_r", [nl, 4, 128, 1536])
    w_ukv_r = inp("w_ukv_r", [nl, 2, 128, 2048])
    wo_r = inp("wo_r", [nl, NCH, 128, 16, 128])
    outT = nc.dram_tensor("outT", [D, T], F32, kind="ExternalOutput").ap()
    xs = nc.dram_tensor("xs", [D, T], F32).ap()
    dk = "ExternalOutput" if cfg.get("dbg_out") else "Internal"
    cmask = inp("cmask", [128, 2048])
    D_ = {
        "cmask": cmask,
        "cqs": nc.dram_tensor("cqs", [512, T], BF16, kind=dk).ap(),
        "gsd": nc.dram_tensor("gsd", [1024, T], BF16, kind=dk).ap(),
        "ys": nc.dram_tensor("ys", [2048, T], BF16, kind=dk).ap(),
        "xkv": nc.dram_tensor("xkv", [256, T], BF16).ap(),
        "xpe": nc.dram_tensor("xpe", [64, T], BF16).ap(),
        "xst": nc.dram_tensor("xst", [512, 256], BF16).ap(),
        "gkv": nc.dram_tensor("gkv", [512, T], BF16).ap(),
        "gpe": nc.dram_tensor("gpe", [128, T], BF16).ap(),
        "gst": nc.dram_tensor("gst", [1024, 256], BF16).ap(),
    }
    group = [[2 * i, 2 * i + 1] for i in range(ncores // 2)]
    with contextlib.ExitStack() as stack:
        P = Prog(nc, stack, list(range(nl)), T)
        P.mix_stop = cfg.get("mix_stop")
        P.consts()
        mix_consts(P, cst, flag, sm_c)
        stages = [(group, ada_in, ada_g1)]
        if ada_n == 4:
            stages.append(([[i, i + 4] for i in range(4)], ada_g1, ada_g))
        P.adaln(ccol_all, onehot, w_ada_r, b_ada_c, ng_c, ada_in, ada_g, ada_n, stages)
        rope_tables(P, pos)
        fcol = P.sb(stack, "fcol", [128, 16], F32)
        P.S.dma("sp", fcol[:], fin_c, writes=[P.b_cst], sem=P.b_cst)
        done = False
        for li in range(nl):
            src = xT if li == 0 else xs
            if stop == ("ffn1", li):
                P.ffn(li, 0, src, outT, wgu_r[li, 0], wd_r[li, 0])
                break
            P.ffn(li, 0, src, xs, wgu_r[li, 0], wd_r[li, 0])
            mixer(P, li, xs, w_inF[li], w_inV[li], w_uq_r[li], w_ukv_r[li], wo_r[li], D_, group)
            if stop == ("mixer", li):
                P.ffn_copy(xs, outT)
                break
            last = (li == nl - 1) and stop is None
            if stop == ("layer", li):
                P.ffn(li, 1, xs, outT, wgu_r[li, 1], wd_r[li, 1])
                break
            P.ffn(li, 1, xs, xs, wgu_r[li, 1], wd_r[li, 1], final=(fcol, outT) if last else None)
        P.S.finish([])
    return nc


def _copy_phase(P, src, dst):
    S = P.S
    with contextlib.ExitStack() as st:
        t = [P.sb(st, f"cp{i}", [128, TB * 4], F32) for i in range(2)]
        bt = S.bufs("cp", 2)
        n = 0
        for c in range(NCH):
            for t0 in range(0, P.T, TB * 4):
                w = min(TB * 4, P.T - t0)
                S.dma("sp", t[n % 2][:, :w], src[c * 128:(c + 1) * 128, t0:t0 + w], writes=[bt[n % 2]])
                S.dma("sp", dst[c * 128:(c + 1) * 128, t0:t0 + w], t[n % 2][:, :w], reads=[bt[n % 2]], sem=bt[n % 2])
                n += 1
        S.barrier()


Prog.ffn_copy = _copy_phase


def _cols_swapped(base):
    idx = []
    for h in range(2):
        b = base + h * 64
        idx += list(range(b + 32, b + 64)) + list(range(b, b + 32))
    return idx


def prep_shared(inp, layers):
    f = lambda a: np.ascontiguousarray(np.asarray(a, dtype=np.float32))
    out = {}
    ls = list(layers)
    out["cst"] = host_consts()
    out["cmask"] = host_cmask()
    out["b_ada_c"] = f(np.stack([np.asarray(inp["b_ada"][l]).reshape(NMOD, 128).T for l in ls]))
    out["ng_c"] = f(np.stack([np.stack([np.asarray(inp[n][l]).reshape(16, 128).T for n in ("norm_ffn1", "norm_mix", "norm_ffn2")])
                              for l in ls]))
    out["fin_c"] = f(np.asarray(inp["final_norm"]).reshape(16, 128).T)
    sm = np.zeros((len(ls), 128, 16), np.float32)
    for i, l in enumerate(ls):
        sm[i, :, 0:8] = np.asarray(inp["ret_norm_g"][l]).reshape(8, 128).T
        sm[i, :, 8:12] = np.asarray(inp["q_norm_g"][l]).reshape(4, 128).T
        sm[i, :, 12:14] = np.asarray(inp["kv_norm_g"][l]).reshape(2, 128).T
    out["sm_c"] = sm
    wg, wdn, wF, wV, wuq, wukv, wo = [], [], [], [], [], [], []
    for l in ls:
        a, b_ = [], []
        for nm_gu, nm_d in (("ffn1_w_gu", "ffn1_w_down"), ("ffn2_w_gu", "ffn2_w_down")):
            w = np.asarray(inp[nm_gu][l])
            g = w[:, :DFF].reshape(16, 128, NFF, 128)
            u = w[:, DFF:].reshape(16, 128, NFF, 128)
            gu = np.concatenate([g, u], axis=3)
            a.append(gu.transpose(2, 1, 0, 3))
            wdd = np.asarray(inp[nm_d][l]).reshape(2, NFF // 2, 128, 16, 128)
            b_.append(wdd.transpose(3, 0, 2, 1, 4))
        wg.append(np.stack(a))
        wdn.append(np.stack(b_))
        wi = np.asarray(inp["w_in"][l])
        colsets = []
        for sec in (0, 512):
            for hp in range(4):
                base = sec + hp * 128
                colsets.append((list(range(base, base + 128)), _cols_swapped(base)))
        for i in range(4):
            colsets.append((list(range(2048 + 2 * i * 128, 2048 + (2 * i + 1) * 128)),
                            list(range(2048 + (2 * i + 1) * 128, 2048 + (2 * i + 2) * 128))))
        for i in range(2):
            colsets.append((list(range(3072 + 2 * i * 128, 3072 + (2 * i + 1) * 128)),
                            list(range(3072 + (2 * i + 1) * 128, 3072 + (2 * i + 2) * 128))))
        colsets.append((list(range(3584, 3712)), list(range(3712, 3840))))
        kp = list(range(3840, 3904))
        kps = list(range(3872, 3904)) + list(range(3840, 3872))
        colsets.append((kp + kp, kps + kps))
        slabs = []
        for c0, c1 in colsets:
            two = []
            for cs in (c0, c1):
                two.append(wi[:, cs].reshape(16, 128, 128).transpose(1, 0, 2))
            slabs.append(np.stack(two, axis=1))
        wF.append(np.stack(slabs))
        wV.append(np.stack([wi[:, 1024 + vs * 256: 1024 + (vs + 1) * 256].reshape(16, 128, 256).transpose(1, 0, 2) for vs in range(4)]))
        wuq.append(np.asarray(inp["w_uq"][l]).reshape(4, 128, 1536))
        wukv.append(np.asarray(inp["w_ukv"][l]).reshape(2, 128, 2048))
        wo_l = np.asarray(inp["w_out"][l]).reshape(16, 128, 16, 128)
        wo.append(wo_l.transpose(2, 1, 0, 3))
    out["wgu_r"] = f(np.stack(wg))
    out["wd_r"] = f(np.stack(wdn))
    out["w_inF"] = f(np.stack(wF))
    out["w_inV"] = f(np.stack(wV))
    out["w_uq_r"] = f(np.stack(wuq))
    out["w_ukv_r"] = f(np.stack(wukv))
    out["wo_r"] = f(np.stack(wo))
    return out


def prep_core(inp, core, T=TOK, ncores=8, layers=(0, 1)):
    b, half = core // 2, core % 2
    x = np.asarray(inp["x"])[b, half * TOK: half * TOK + T, :]
    ada_n = 4 if ncores == 8 else 2
    CW = NMOD * 128 // ada_n
    sl_i = (core // 4) * 2 + core % 2 if ncores == 8 else core % 2
    oh = np.zeros((128, 4), np.float32)
    oh[:, b] = 1.0
    d = {"xT": np.ascontiguousarray(x.T.astype(np.float32)),
         "ccol_all": np.ascontiguousarray(np.asarray(inp["c"], dtype=np.float32).reshape(4, 16, 128).transpose(2, 1, 0)),
         "onehot": oh,
         "w_ada_r": np.ascontiguousarray(np.stack([np.asarray(inp["w_ada"][l])[:, sl_i * CW:(sl_i + 1) * CW]
                                                   .reshape(16, 128, CW).transpose(1, 0, 2) for l in layers]).astype(np.float32)),
         "pos": np.ascontiguousarray(np.asarray(inp["positions"])[b, half * TOK: half * TOK + T].reshape(1, T).astype(np.int32)),
         "flag": np.full((128, 1), float(half), np.float32)}
    return d


_NC_CACHE = {}


def kernel(**inputs):
    ncores = 8
    if "nc" not in _NC_CACHE:
        _NC_CACHE["nc"] = build({"nl": L, "T": TOK, "ncores": ncores})
    nc = _NC_CACHE["nc"]
    sh = prep_shared(inputs, range(L))
    maps = []
    for c in range(ncores):
        m = dict(sh)
        m.update(prep_core(inputs, c, TOK, ncores, range(L)))
        maps.append(m)
    res = run_bass_kernel_spmd(nc, maps, core_ids=list(range(ncores)))
    x = np.asarray(inputs["x"])
    out = np.empty(x.shape, np.float32)
    for c in range(ncores):
        b, half = c // 2, c % 2
        out[b, half * TOK:(half + 1) * TOK, :] = np.asarray(res.results[c]["outT"]).T
    return out
```
